# Optimizing a Trainium2 kernel written in Bass

```python
import math
import jax, jax.numpy as jnp
from jax import lax
import numpy as np

D_MODEL = 1024
BATCH = 8
SEQ = 4096
DEPTH = 2

N_A_LAYERS = DEPTH // 2
N_B_LAYERS = DEPTH - N_A_LAYERS

GDN_HEADS = 8
GDN_DK = 128
GDN_DV = 128
GDN_QK_WIDTH = GDN_HEADS * GDN_DK
GDN_V_WIDTH = GDN_HEADS * GDN_DV
GDN_CONV = 4
GDN_CHUNK = 64
GDN_IN_WIDTH = 2 * GDN_QK_WIDTH + 2 * GDN_V_WIDTH + 2 * GDN_HEADS

FOX_HEADS = 16
FOX_HD = 64
FOX_WIDTH = FOX_HEADS * FOX_HD
FOX_QBLOCK = 128
KV_WIDTH = 2 * FOX_WIDTH + FOX_HEADS

D_FF = int(math.ceil(8 * D_MODEL / 3 / 256) * 256)

RMS_EPS = 1e-6

kernel_name = "yoco_gdn_fox_hybrid"


def rms_norm(x, w):
    xf = x.astype(jnp.float32)
    y = xf * lax.rsqrt(jnp.mean(xf * xf, axis=-1, keepdims=True) + RMS_EPS)
    return (y * w.astype(jnp.float32)).astype(x.dtype)


def l2_normalize(x):
    xf = x.astype(jnp.float32)
    return xf * lax.rsqrt(jnp.sum(xf * xf, axis=-1, keepdims=True) + RMS_EPS)


def causal_depthwise_conv_silu(x, w):
    k_width = w.shape[0]
    y = lax.conv_general_dilated(
        x, w[:, None, :].astype(x.dtype), window_strides=(1,), padding=[(k_width - 1, 0)],
        dimension_numbers=("NWC", "WIO", "NWC"), feature_group_count=x.shape[-1])
    return jax.nn.silu(y)


def gated_delta_rule_chunked(q, k, v, g, beta):
    b_sz, t_len, n_h, d_k = q.shape
    d_v = v.shape[-1]
    c = GDN_CHUNK
    n_c = t_len // c

    def to_chunks(a):
        return a.reshape(b_sz, n_c, c, n_h, *a.shape[3:]).swapaxes(2, 3)

    qc, kc, vc, gc, bc = map(to_chunks, (q, k, v, g, beta))
    G = jnp.cumsum(gc, axis=-1)
    causal = jnp.tril(jnp.ones((c, c), dtype=bool))
    strict = jnp.tril(jnp.ones((c, c), dtype=bool), -1)
    diff = G[..., :, None] - G[..., None, :]
    decay_mat = jnp.where(causal, jnp.exp(jnp.where(causal, diff, 0.0)), 0.0)

    kk = jnp.einsum("bnhcd,bnhsd->bnhcs", kc, kc)
    a_strict = jnp.where(strict, bc[..., :, None] * kk * decay_mat, 0.0)
    rhs = jnp.concatenate([bc[..., None] * kc * jnp.exp(G)[..., None], bc[..., None] * vc], axis=-1)
    sol = lax.linalg.triangular_solve(a_strict, rhs, left_side=True, lower=True, unit_diagonal=True)
    w_c, u_c = sol[..., :d_k], sol[..., d_k:]

    a_qk = jnp.einsum("bnhcd,bnhsd->bnhcs", qc, kc) * decay_mat
    g_last = G[..., -1]
    q_dec = qc * jnp.exp(G)[..., None]
    k_dec = kc * jnp.exp(g_last[..., None] - G)[..., None]

    xs = tuple(jnp.moveaxis(a, 1, 0) for a in (q_dec, k_dec, u_c, w_c, a_qk, jnp.exp(g_last)))

    def step(state, inp):
        qg, kd, u_i, w_i, a_i, gl = inp
        v_new = u_i - jnp.einsum("bhcd,bhde->bhce", w_i, state)
        o = jnp.einsum("bhcd,bhde->bhce", qg, state) + jnp.einsum("bhcs,bhse->bhce", a_i, v_new)
        state = state * gl[..., None, None] + jnp.einsum("bhcd,bhce->bhde", kd, v_new)
        return state, o

    s0 = jnp.zeros((b_sz, n_h, d_k, d_v), jnp.float32)
    _, o = lax.scan(step, s0, xs)
    return o.transpose(1, 0, 3, 2, 4).reshape(b_sz, t_len, n_h, d_v)


def gated_deltanet_mixer(h, w_in, conv_w, a_log, dt_bias, out_norm, w_out):
    b_sz, t_len, _ = h.shape
    proj = h @ w_in
    o1 = 2 * GDN_QK_WIDTH + GDN_V_WIDTH
    o2 = o1 + GDN_V_WIDTH
    qkv = causal_depthwise_conv_silu(proj[..., :o1], conv_w)
    z = proj[..., o1:o2].reshape(b_sz, t_len, GDN_HEADS, GDN_DV)
    b_logit = proj[..., o2:o2 + GDN_HEADS]
    a_in = proj[..., o2 + GDN_HEADS:]
    q = qkv[..., :GDN_QK_WIDTH].reshape(b_sz, t_len, GDN_HEADS, GDN_DK)
    k = qkv[..., GDN_QK_WIDTH:2 * GDN_QK_WIDTH].reshape(b_sz, t_len, GDN_HEADS, GDN_DK)
    v = qkv[..., 2 * GDN_QK_WIDTH:].reshape(b_sz, t_len, GDN_HEADS, GDN_DV).astype(jnp.float32)
    q = l2_normalize(q) * (GDN_DK ** -0.5)
    k = l2_normalize(k)
    beta = jax.nn.sigmoid(b_logit.astype(jnp.float32))
    g = -jnp.exp(a_log.astype(jnp.float32)) * jax.nn.softplus(
        a_in.astype(jnp.float32) + dt_bias.astype(jnp.float32))
    o = gated_delta_rule_chunked(q, k, v, g, beta).astype(h.dtype)
    o = rms_norm(o, out_norm) * jax.nn.silu(z)
    return o.reshape(b_sz, t_len, GDN_V_WIDTH) @ w_out


def shared_kv(x, kv_norm, w_kv, b_forget):
    b_sz, t_len, _ = x.shape
    kvf = rms_norm(x, kv_norm) @ w_kv
    k = kvf[..., :FOX_WIDTH].reshape(b_sz, t_len, FOX_HEADS, FOX_HD).transpose(0, 2, 1, 3)
    v = kvf[..., FOX_WIDTH:2 * FOX_WIDTH].reshape(b_sz, t_len, FOX_HEADS, FOX_HD).transpose(0, 2, 1, 3)
    f_logit = (kvf[..., 2 * FOX_WIDTH:] + b_forget).astype(jnp.float32)
    log_f_cum = jnp.cumsum(jax.nn.log_sigmoid(f_logit), axis=1).transpose(0, 2, 1)
    return k, v, log_f_cum


def forgetting_attention_mixer(h, w_q, w_o, k, v, log_f_cum):
    b_sz, t_len, _ = h.shape
    q = (h @ w_q).reshape(b_sz, t_len, FOX_HEADS, FOX_HD).transpose(0, 2, 1, 3)
    scale = FOX_HD ** -0.5
    outs = []
    for blk in range(t_len // FOX_QBLOCK):
        s0 = blk * FOX_QBLOCK
        e = s0 + FOX_QBLOCK
        s = jnp.einsum("bhqd,bhkd->bhqk", q[:, :, s0:e], k[:, :, :e]).astype(jnp.float32) * scale
        s = s + log_f_cum[:, :, s0:e, None] - log_f_cum[:, :, None, :e]
        mask = (s0 + jnp.arange(FOX_QBLOCK))[:, None] >= jnp.arange(e)[None, :]
        p = jax.nn.softmax(jnp.where(mask, s, -jnp.inf), axis=-1)
        outs.append(jnp.einsum("bhqk,bhkd->bhqd", p.astype(v.dtype), v[:, :, :e]))
    o = jnp.concatenate(outs, axis=2)
    return o.transpose(0, 2, 1, 3).reshape(b_sz, t_len, FOX_WIDTH) @ w_o


def swiglu(h, w_in, w_out):
    gu = h @ w_in
    return (jax.nn.silu(gu[..., :D_FF]) * gu[..., D_FF:]) @ w_out


def setup_inputs(seed: int = 0) -> dict:
    key = jax.random.key(seed)
    ks = jax.random.split(key, 20)
    f32 = jnp.float32

    def w(k, shape, fan_in):
        return jax.random.normal(k, shape, f32) * (fan_in ** -0.5)

    def gain(k, shape):
        return 1.0 + 0.1 * jax.random.normal(k, shape, f32)

    x = jax.random.normal(ks[0], (BATCH, SEQ, D_MODEL), f32)
    a_init = jax.random.uniform(ks[9], (N_A_LAYERS, GDN_HEADS), f32, 1.0, 16.0)
    dt = jnp.exp(jax.random.uniform(ks[10], (N_A_LAYERS, GDN_HEADS), f32, math.log(1e-3), math.log(0.1)))
    return {
        "x": x,
        "pre_mix_norm": gain(ks[1], (DEPTH, D_MODEL)),
        "post_mix_norm": gain(ks[2], (DEPTH, D_MODEL)),
        "pre_ffn_norm": gain(ks[3], (DEPTH, D_MODEL)),
        "post_ffn_norm": gain(ks[4], (DEPTH, D_MODEL)),
        "w_ffn_in": w(ks[5], (DEPTH, D_MODEL, 2 * D_FF), D_MODEL),
        "w_ffn_out": w(ks[6], (DEPTH, D_FF, D_MODEL), D_FF),
        "gdn_w_in": w(ks[7], (N_A_LAYERS, D_MODEL, GDN_IN_WIDTH), D_MODEL),
        "gdn_conv": w(ks[8], (N_A_LAYERS, GDN_CONV, 2 * GDN_QK_WIDTH + GDN_V_WIDTH), GDN_CONV),
        "gdn_a_log": jnp.log(a_init),
        "gdn_dt_bias": dt + jnp.log(-jnp.expm1(-dt)),
        "gdn_out_norm": gain(ks[11], (N_A_LAYERS, GDN_DV)),
        "gdn_w_out": w(ks[12], (N_A_LAYERS, GDN_V_WIDTH, D_MODEL), GDN_V_WIDTH),
        "kv_norm": gain(ks[13], (D_MODEL,)),
        "w_kv": w(ks[14], (D_MODEL, KV_WIDTH), D_MODEL),
        "b_forget": 2.0 + 0.5 * jax.random.normal(ks[15], (FOX_HEADS,), f32),
        "fox_w_q": w(ks[16], (N_B_LAYERS, D_MODEL, FOX_WIDTH), D_MODEL),
        "fox_w_o": w(ks[17], (N_B_LAYERS, FOX_WIDTH, D_MODEL), FOX_WIDTH),
    }


def reference(x, pre_mix_norm, post_mix_norm, pre_ffn_norm, post_ffn_norm, w_ffn_in, w_ffn_out,
              gdn_w_in, gdn_conv, gdn_a_log, gdn_dt_bias, gdn_out_norm, gdn_w_out,
              kv_norm, w_kv, b_forget, fox_w_q, fox_w_o):
    k_sh = v_sh = c_sh = None
    for layer in range(DEPTH):
        h = rms_norm(x, pre_mix_norm[layer])
        if layer < N_A_LAYERS:
            i = layer
            mix = gated_deltanet_mixer(h, gdn_w_in[i], gdn_conv[i], gdn_a_log[i], gdn_dt_bias[i],
                                       gdn_out_norm[i], gdn_w_out[i])
        else:
            i = layer - N_A_LAYERS
            if i == 0:
                k_sh, v_sh, c_sh = shared_kv(x, kv_norm, w_kv, b_forget)
            mix = forgetting_attention_mixer(h, fox_w_q[i], fox_w_o[i], k_sh, v_sh, c_sh)
        x = x + rms_norm(mix, post_mix_norm[layer])
        f = swiglu(rms_norm(x, pre_ffn_norm[layer]), w_ffn_in[layer], w_ffn_out[layer])
        x = x + rms_norm(f, post_ffn_norm[layer])
    return x
```

```python
import contextlib
import numpy as np
import concourse.bass as bass
import concourse.mybir as mybir
from concourse.bass_utils import run_bass_kernel_spmd

F32 = mybir.dt.float32
BF16 = mybir.dt.bfloat16
AF = mybir.ActivationFunctionType
ALU = mybir.AluOpType

SAME_ENGINE_SYNC = True
SEM_ROTATE = 12000
DMA_SEMS_PER_Q = 8

T = 4096
D = 1024
NCORES = 8
DFF = 2816
EPS = 1e-6


class Dep:
    __slots__ = ("w", "r")

    def __init__(self):
        self.w = None
        self.r = []


class _Eng:
    def __init__(self, name):
        self.name = name
        self.ops = []
        self.clock = {}
        self.sem = None
        self.count = 0
        self.dma_sems = []
        self.dma_counts = []
        self.dma_last = []
        self.dma_i = 0


class Prog:
    ENGS = ("tensor", "vector", "scalar", "gpsimd", "sync")

    def __init__(self, nc):
        self.nc = nc
        self.e = {n: _Eng(n) for n in self.ENGS}
        self.sems = []
        self.nsem = 0
        self.nops = 0
        for n in self.ENGS:
            self._new_sem(self.e[n])

    def _alloc_sem(self, name):
        s = self.nc.alloc_semaphore(name)
        self.sems.append(s)
        return len(self.sems) - 1

    def _new_sem(self, E):
        E.sem = self._alloc_sem(f"s_{E.name}_{self.nsem}")
        self.nsem += 1
        E.count = 0

    def _dma_slot(self, E):
        if len(E.dma_sems) < DMA_SEMS_PER_Q:
            E.dma_sems.append(self._alloc_sem(f"d_{E.name}_{len(E.dma_sems)}"))
            E.dma_counts.append(0)
            E.dma_last.append(None)
        i = E.dma_i % DMA_SEMS_PER_Q
        E.dma_i += 1
        return i

    def emit(self, eng, fn, reads=(), writes=(), dma=False):
        E = self.e[eng]
        need = {}

        def req(t):
            if t is None:
                return
            s, v, src, isdma = t
            if (not isdma) and src == eng:
                if eng == "tensor" or not SAME_ENGINE_SYNC:
                    return
            if need.get(s, 0) < v:
                need[s] = v

        for d in reads:
            req(d.w)
        for d in writes:
            req(d.w)
            for t in d.r:
                req(t)
        slot = None
        if dma:
            slot = self._dma_slot(E)
            req(E.dma_last[slot])
        waits = []
        for s, v in need.items():
            if E.clock.get(s, 0) < v:
                waits.append((s, v))
                E.clock[s] = v
        if dma:
            E.dma_counts[slot] += 16
            tk = (E.dma_sems[slot], E.dma_counts[slot], eng, True)
            E.dma_last[slot] = tk
            inc = 16
        else:
            if E.count >= SEM_ROTATE:
                self._new_sem(E)
            E.count += 1
            tk = (E.sem, E.count, eng, False)
            inc = 1
        E.ops.append((waits, fn, tk[0], inc))
        self.nops += 1
        for d in reads:
            d.r.append(tk)
        for d in writes:
            d.w = tk
            d.r = []
        return tk

    def barrier(self):
        last = []
        for E in self.e.values():
            if E.count > 0:
                last.append((E.sem, E.count))
            for t in E.dma_last:
                if t is not None:
                    last.append((t[0], t[1]))
        for F in self.e.values():
            waits = []
            for s, v in last:
                if F.name == "tensor" and s == F.sem:
                    continue
                if F.clock.get(s, 0) < v:
                    waits.append((s, v))
                    F.clock[s] = v
            F.ops.append((waits, None, None, 0))

    def replay(self):
        nc = self.nc
        sems = self.sems
        with nc.Block() as block:
            def mk(name):
                ops = self.e[name].ops

                def body(eng):
                    for waits, fn, s, inc in ops:
                        for ws, wv in waits:
                            eng.wait_ge(sems[ws], wv)
                        if fn is not None:
                            ins = fn(eng)
                            ins.then_inc(sems[s], inc)
                return body
            block.sync(mk("sync"))
            block.tensor(mk("tensor"))
            block.vector(mk("vector"))
            block.scalar(mk("scalar"))
            block.gpsimd(mk("gpsimd"))
        for E in self.e.values():
            E.ops = []


PHASES = ("A1", "A2", "F0", "B01", "B2", "B3", "F1")


def build_program(stop_after=None, debug=False):
    nc = bass.Bass("TRN2", target_bir_lowering=False)
    P = Prog(nc)

    def din(name, shape, dt=F32):
        return nc.dram_tensor(name, list(shape), dt, kind="ExternalInput").ap()

    def dscr(name, shape, dt):
        if debug:
            return nc.dram_tensor(name, list(shape), dt, kind="ExternalOutput").ap()
        return nc.dram_tensor(name, list(shape), dt).ap()

    xT = din("xT", [D, T])
    nrm_d = din("nrm", [128, 72])
    gw_in = din("gw_in", [32, 128, 1024])
    gw_tail = din("gw_tail", [128, 8, 16])
    gconv_d = din("gconv", [128, 24, 4])
    galog_d = din("galog", [128, 8])
    gdtb_d = din("gdtb", [128, 8])
    gonorm_d = din("gonorm", [128, 1])
    gw_out = din("gw_out", [8, 128, 1024])
    wffn_in = din("wffn_in", [2, 44, 128, 1024])
    wffn_out = din("wffn_out", [2, 8, 128, DFF])
    wk_d = din("wk", [8, 128, 1024])
    wv_d = din("wv", [128, 8, 1024])
    wf_d = din("wf", [128, 8, 16])
    bfor_d = din("bfor", [128, 16])
    wq_d = din("wq", [8, 128, 1024])
    wo_d = din("wo", [8, 128, 1024])
    cst_d = din("cst", [128, 7, 128])
    outT = nc.dram_tensor("outT", [D, T], F32, kind="ExternalOutput").ap()

    R1 = dscr("R1", [D, T], F32)
    R2 = dscr("R2", [D, T], F32)
    R3 = dscr("R3", [D, T], F32)
    qT_s = dscr("qT_s", [8, 128, T], F32)
    kT_s = dscr("kT_s", [8, 128, T], F32)
    ktokf_s = dscr("ktokf_s", [32, 128, 8, 128], F32)
    sz_s = dscr("sz_s", [8, 128, T], F32)
    vtokf_s = dscr("vtokf_s", [32, 128, 8, 128], F32)
    kx_s = dscr("kx_s", [16, 67, T], BF16)
    qx_s = dscr("qx_s", [16, 67, T], BF16)
    vx_s = dscr("vx_s", [32, 128, 16, 65], BF16)
    at_s = dscr("at_s", [D, T], BF16)

    banks = [nc.alloc_psum_tensor(f"pb{i}", [128, 512], F32) for i in range(8)]
    bankb = [b[:, :].bitcast(BF16) for b in banks]
    bdep = [Dep() for _ in range(8)]
    rr = [0]

    def nb():
        rr[0] = (rr[0] + 1) % 8
        return rr[0]

    def mm(out, lhsT, rhs, st, sp, r, w):
        P.emit("tensor", lambda e: e.matmul(out, lhsT=lhsT, rhs=rhs, start=st, stop=sp), r, w)

    def tr(out, in_, ident, r, w):
        P.emit("tensor", lambda e: e.transpose(out, in_, ident), r, w)

    def act(out, in_, func, r, w, bias=None, scale=None, accum=None):
        kw = {}
        if bias is not None:
            kw["bias"] = bias
        if scale is not None:
            kw["scale"] = scale
        if accum is not None:
            kw["accum_out"] = accum
        P.emit("scalar", lambda e: e.activation(out=out, in_=in_, func=func, **kw), r, w)

    def vts(eng, out, in0, s1, op0, r, w, s2=None, op1=None):
        if op1 is None:
            P.emit(eng, lambda e: e.tensor_scalar(out=out, in0=in0, scalar1=s1, scalar2=None, op0=op0), r, w)
        else:
            P.emit(eng, lambda e: e.tensor_scalar(out=out, in0=in0, scalar1=s1, scalar2=s2, op0=op0, op1=op1), r, w)

    def stt(eng, out, in0, scalar, in1, op0, op1, r, w):
        P.emit(eng, lambda e: e.scalar_tensor_tensor(out=out, in0=in0, scalar=scalar, in1=in1, op0=op0, op1=op1), r, w)

    def tt(eng, out, in0, in1, op, r, w):
        P.emit(eng, lambda e: e.tensor_tensor(out=out, in0=in0, in1=in1, op=op), r, w)

    def cp(eng, out, in_, r, w):
        if eng == "scalar":
            act(out, in_, AF.Copy, r, w)
        else:
            P.emit(eng, lambda e: e.tensor_copy(out=out, in_=in_), r, w)

    def recip(out, in_, r, w):
        P.emit("vector", lambda e: e.reciprocal(out=out, in_=in_), r, w)

    def mset(eng, ap, val, w):
        P.emit(eng, lambda e: e.memset(ap, val), [], w)

    def dma(q, out, in_, r, w):
        P.emit(q, lambda e: e.dma_start(out=out, in_=in_), r, w, dma=True)

    ddram = Dep()

    def gsb(name, shape, dt):
        return nc.alloc_sbuf_tensor("g_" + name, list(shape), dt)

    cst = gsb("cst", [128, 7, 128], F32)
    dcst = Dep()
    ident_f, ltri, ustr, ones_f, negmask, strict01, incl01 = [cst[:, i, :] for i in range(7)]
    cstb = gsb("cstb", [128, 3, 128], BF16)
    ident_b, ones_b, incl01_b = cstb[:, 0, :], cstb[:, 1, :], cstb[:, 2, :]
    nrm = gsb("nrm", [128, 72], F32)
    g_all = gsb("g_all", [128, 32, 8], F32)
    ng_all = gsb("ng_all", [128, 32, 8], F32)
    beta_all = gsb("beta_all", [128, 32, 8], F32)
    nbeta_all = gsb("nbeta_all", [128, 32, 8], F32)
    eg_all = gsb("eg_all", [128, 32, 24], F32)
    negc_all = gsb("negc_all", [128, 32, 16], F32)
    dsmall = Dep()

    dma("sync", cst[:, :, :], cst_d[:, :, :], [], [dcst])
    dma("sync", nrm[:, :], nrm_d[:, :], [], [dcst])
    cp("vector", cstb[:, 0, :], cst[:, 0, :], [dcst], [dcst])
    cp("vector", cstb[:, 1, :], cst[:, 3, :], [dcst], [dcst])
    cp("vector", cstb[:, 2, :], cst[:, 6, :], [dcst], [dcst])

    def nw(v, c):
        return nrm[:, v * 8 + c: v * 8 + c + 1]

    def rms_rstd(xs, dxs, W, sq, dsq, sdt, rstd, dst_, scale):
        n = len(xs)
        b = nb()
        for c in range(n):
            act(sq[c][:, :W], xs[c], AF.Square, [dxs[c]], [dsq[c]])
        for c in range(n):
            mm(banks[b][:, :W], ones_b, sq[c][:, :W], c == 0, c == n - 1, [dsq[c], dcst], [bdep[b]])
        act(sdt[:, :W], banks[b][:, :W], AF.Ln, [], [bdep[b], dst_], bias=EPS, scale=scale)
        act(rstd[:, :W], sdt[:, :W], AF.Exp, [dst_], [dst_], scale=-0.5)
        return rstd[:, :W]

    def post_norm_residual(ys, dys, W, nv, xres, dxres, res_out_dram, col0, sq, dsq, sdt, rstd, dst_, tmp, dtmp):
        r = rms_rstd([y[:, :W] for y in ys], dys, W, sq, dsq, sdt, rstd, dst_, 1.0 / D)
        for c in range(8):
            tt("vector", tmp[c][:, :W], ys[c][:, :W], r, ALU.mult, [dys[c], dst_], [dtmp[c]])
            stt("vector", tmp[c][:, :W], tmp[c][:, :W], nw(nv, c), xres[c][:, :W], ALU.mult, ALU.add, [dtmp[c], dxres[c], dcst], [dtmp[c]])
            dma("sync", res_out_dram[c * 128:(c + 1) * 128, col0:col0 + W], tmp[c][:, :W], [dtmp[c]], [Dep()])

    phase_no = [0]

    def run_phase(fn):
        phase_no[0] += 1
        pfx = f"p{phase_no[0]}_"
        with contextlib.ExitStack() as es:
            def sb(name, shape, dt):
                return es.enter_context(nc.sbuf_tensor(pfx + name, list(shape), dt))
            fn(sb)
            P.barrier()
            P.replay()

    def run_pipeline(chains, nstage):
        n = len(chains)
        for t in range(n + nstage - 1):
            for st in range(nstage):
                c = t - st
                if 0 <= c < n and st < len(chains[c]) and chains[c][st] is not None:
                    chains[c][st]()

    def phase_A1(sb):
        W = sb("a1w", [128, 32, 1024], BF16)
        Wd = [Dep() for _ in range(32)]
        for oc in range(32):
            dma("gpsimd", W[:, oc, :], gw_in[oc], [], [Wd[oc]])
        wt = sb("a1wt", [128, 8, 16], BF16)
        dwt = Dep()
        dma("gpsimd", wt[:, :, :], gw_tail[:, :, :], [], [dwt])
        cw = sb("a1cw", [128, 24, 4], F32)
        alog = sb("a1alog", [128, 8], F32)
        dtb = sb("a1dtb", [128, 8], F32)
        negA = sb("a1negA", [128, 8], F32)
        posA = sb("a1posA", [128, 8], F32)
        dc = Dep()
        dma("sync", cw[:, :, :], gconv_d[:, :, :], [], [dc])
        dma("sync", alog[:, :], galog_d[:, :], [], [dc])
        dma("sync", dtb[:, :], gdtb_d[:, :], [], [dc])
        act(posA[:, :], alog[:, :], AF.Exp, [dc], [dc])
        vts("vector", negA[:, :], posA[:, :], -1.0, ALU.mult, [dc], [dc])
        halo = sb("a1halo", [128, 24, 3], F32)
        dhalo = [Dep() for _ in range(24)]
        mset("gpsimd", halo[:, :, :], 0.0, dhalo)
        xs = [sb(f"a1x{c}", [128, 512], F32) for c in range(8)]
        dxs = [Dep() for c in range(8)]
        hb = [[sb(f"a1hb{s}_{c}", [128, 512], BF16) for c in range(8)] for s in range(2)]
        dhb = [[Dep() for _ in range(8)] for s in range(2)]
        sq = [sb(f"a1sq{c}", [128, 512], BF16) for c in range(8)]
        dsq = [Dep() for _ in range(8)]
        sdt = sb("a1sdt", [128, 512], F32)
        rstd = sb("a1rstd", [128, 512], F32)
        dst_ = Dep()

        def ring(name, n, shape, dt):
            return [sb(f"{name}{i}", shape, dt) for i in range(n)], [Dep() for _ in range(n)]
        NCB, NACC, NY, NSQ1, NSD1, NOF, NTK = 5, 6, 6, 3, 4, 3, 3
        cbuf, dcb = ring("a1cb", NCB, [128, 515], F32)
        acc, dacc = ring("a1acc", NACC, [128, 512], F32)
        yb, dyb = ring("a1y", NY, [128, 512], F32)
        sq1, dsq1 = ring("a1sq1", NSQ1, [128, 512], BF16)
        sd1, dsd1 = ring("a1sd1", NSD1, [128, 512], F32)
        of, dof = ring("a1of", NOF, [128, 512], F32)
        szt, dszt = ring("a1sz", 2, [128, 512], F32)
        tokf, dtokf = ring("a1tokf", NTK, [128, 4, 128], F32)
        tl = sb("a1tl", [128, 4, 8], F32)
        dtl = [Dep() for _ in range(4)]
        cnt = {"cb": 0, "acc": 0, "y": 0, "sq1": 0, "sd1": 0, "of": 0, "sz": 0, "tk": 0}

        def take(k, n):
            v = cnt[k] % n
            cnt[k] += 1
            return v

        chains = []
        tile_heads = {}
        for j in range(8):
            jb = j % 2
            c0 = j * 512

            def tile_head(j=j, jb=jb, c0=c0):
                for c in range(8):
                    dma("sync", xs[c][:, :], xT[c * 128:(c + 1) * 128, c0:c0 + 512], [], [dxs[c]])
                r = rms_rstd([x[:, :] for x in xs], dxs, 512, sq, dsq, sdt, rstd, dst_, 1.0 / D)
                for c in range(8):
                    stt("vector", hb[jb][c][:, :], xs[c][:, :], nw(0, c), r, ALU.mult, ALU.mult, [dxs[c], dst_, dcst], [dhb[jb][c]])

            tile_heads[j] = tile_head
            for oc in range(32):
                st = {}

                def s0(oc=oc, j=j, jb=jb, st=st, first=(oc == 0 and j == 0), th=tile_head):
                    if first:
                        th()
                    if oc == 20 and j + 1 < 8:
                        tile_heads[j + 1]()
                    b = nb()
                    st["b"] = b
                    for kc in range(8):
                        mm(banks[b][:, :], W[:, oc, kc * 128:(kc + 1) * 128], hb[jb][kc][:, :], kc == 0, kc == 7, [Wd[oc], dhb[jb][kc]], [bdep[b]])

                if oc >= 24:
                    def s1(oc=oc, c0=c0, st=st):
                        b = st["b"]
                        u = take("sz", 2)
                        act(szt[u][:, :], banks[b][:, :], AF.Silu, [], [bdep[b], dszt[u]])
                        dma("sync", sz_s[oc - 24, :, c0:c0 + 512], szt[u][:, :], [dszt[u]], [Dep()])
                    chains.append([s0, s1])
                    continue

                def s1(oc=oc, st=st):
                    b = st["b"]
                    u = take("cb", NCB)
                    a = take("acc", NACC)
                    st["cb"], st["acc"] = u, a
                    cb = cbuf[u]
                    cp("gpsimd", cb[:, 0:3], halo[:, oc, :], [dhalo[oc]], [dcb[u]])
                    act(cb[:, 3:515], banks[b][:, :], AF.Copy, [], [bdep[b], dcb[u]])
                    act(acc[a][:, :], banks[b][:, :], AF.Copy, [dc], [bdep[b], dacc[a]], scale=cw[:, oc, 3:4])
                    cp("gpsimd", halo[:, oc, :], cb[:, 512:515], [dcb[u]], [dhalo[oc]])

                def mk_tap(k, oc=oc, st=st):
                    def f():
                        u, a = st["cb"], st["acc"]
                        stt("vector", acc[a][:, :], cbuf[u][:, k:k + 512], cw[:, oc, k:k + 1], acc[a][:, :], ALU.mult, ALU.add, [dcb[u], dc], [dacc[a]])
                    return f

                def s5(oc=oc, st=st):
                    a = st["acc"]
                    yv = take("y", NY)
                    st["y"] = yv
                    act(yb[yv][:, :], acc[a][:, :], AF.Silu, [dacc[a]], [dyb[yv]])
                    if oc < 16:
                        q1 = take("sq1", NSQ1)
                        st["sq1"] = q1
                        act(sq1[q1][:, :], yb[yv][:, :], AF.Square, [dyb[yv]], [dsq1[q1]])

                def s6(oc=oc, st=st):
                    if oc >= 16:
                        return
                    q1 = st["sq1"]
                    d1 = take("sd1", NSD1)
                    st["sd1"] = d1
                    b2 = nb()
                    mm(banks[b2][:, :], ones_b, sq1[q1][:, :], True, True, [dsq1[q1], dcst], [bdep[b2]])
                    act(sd1[d1][:, :], banks[b2][:, :], AF.Ln, [], [bdep[b2], dsd1[d1]], bias=EPS, scale=1.0)

                def s7(oc=oc, st=st):
                    if oc >= 16:
                        return
                    d1 = st["sd1"]
                    act(sd1[d1][:, :], sd1[d1][:, :], AF.Exp, [dsd1[d1]], [dsd1[d1]], scale=-0.5)

                def s8(oc=oc, st=st):
                    if oc >= 16:
                        return
                    yv, d1 = st["y"], st["sd1"]
                    o_ = take("of", NOF)
                    st["of"] = o_
                    sc = (128.0 ** -0.5) if oc < 8 else 1.0
                    stt("vector", of[o_][:, :], yb[yv][:, :], sc, sd1[d1][:, :], ALU.mult, ALU.mult, [dyb[yv], dsd1[d1]], [dof[o_]])

                def s9(oc=oc, j=j, c0=c0, st=st):
                    h = oc % 8
                    if oc < 16:
                        o_ = st["of"]
                        dst_d = qT_s if oc < 8 else kT_s
                        dma("sync", dst_d[h, :, c0:c0 + 512], of[o_][:, :], [dof[o_]], [Dep()])
                        src, dsrc = of[o_], dof[o_]
                    else:
                        yv = st["y"]
                        src, dsrc = yb[yv], dyb[yv]
                    if oc >= 8:
                        b3 = nb()
                        for i in range(4):
                            tr(banks[b3][:, i * 128:(i + 1) * 128], src[:, i * 128:(i + 1) * 128], ident_f, [dsrc, dcst], [bdep[b3]])
                        tk = take("tk", NTK)
                        if oc < 16:
                            cp("vector", tokf[tk][:, :, :], banks[b3][:, :].rearrange("p (i d) -> p i d", i=4), [], [bdep[b3], dtokf[tk]])
                            dma("sync", ktokf_s[4 * j:4 * j + 4, :, h, :].rearrange("i p d -> p i d"), tokf[tk][:, :, :], [dtokf[tk]], [Dep()])
                        else:
                            cp("scalar", tokf[tk][:, :, :], banks[b3][:, :].rearrange("p (i d) -> p i d", i=4), [], [bdep[b3], dtokf[tk]])
                            dma("sync", vtokf_s[4 * j:4 * j + 4, :, h, :].rearrange("i p d -> p i d"), tokf[tk][:, :, :], [dtokf[tk]], [Dep()])

                chains.append([s0, s1, mk_tap(2), mk_tap(1), mk_tap(0), s5, s6, s7, s8, s9])

            for i in range(4):
                ti = 4 * j + i
                st = {}

                def g0(i=i, jb=jb, st=st):
                    b = nb()
                    st["b"] = b
                    for kc in range(8):
                        mm(banks[b][:, 0:16], hb[jb][kc][:, i * 128:(i + 1) * 128], wt[:, kc, :], kc == 0, kc == 7, [dhb[jb][kc], dwt], [bdep[b]])

                def g1(i=i, ti=ti, st=st):
                    b = st["b"]
                    act(beta_all[:, ti, :], banks[b][:, 0:8], AF.Sigmoid, [], [bdep[b], dsmall])
                    tt("vector", tl[:, i, :], banks[b][:, 8:16], dtb[:, :], ALU.add, [dc], [bdep[b], dtl[i]])
                    vts("gpsimd", nbeta_all[:, ti, :], beta_all[:, ti, :], -1.0, ALU.mult, [dsmall], [dsmall])

                def g2(i=i, st=st):
                    act(tl[:, i, :], tl[:, i, :], AF.Exp, [dtl[i]], [dtl[i]])
                    act(tl[:, i, :], tl[:, i, :], AF.Ln, [dtl[i]], [dtl[i]], bias=1.0)

                def g3(i=i, ti=ti, st=st):
                    tt("vector", g_all[:, ti, :], tl[:, i, :], negA[:, :], ALU.mult, [dtl[i], dc], [dsmall])
                    tt("vector", ng_all[:, ti, :], tl[:, i, :], posA[:, :], ALU.mult, [dtl[i], dc], [dsmall])
                    b2 = nb()
                    st["b2"] = b2
                    mm(banks[b2][:, 0:8], ltri, g_all[:, ti, :], True, True, [dsmall, dcst], [bdep[b2]])
                    mm(banks[b2][:, 8:16], ustr, g_all[:, ti, :], True, True, [dsmall, dcst], [bdep[b2]])
                    mm(banks[b2][:, 16:24], ones_f, g_all[:, ti, :], True, True, [dsmall, dcst], [bdep[b2]])

                def g4(ti=ti, st=st):
                    b2 = st["b2"]
                    act(eg_all[:, ti, :], banks[b2][:, 0:24], AF.Exp, [], [bdep[b2], dsmall])

                chains.append([g0, g1, g2, g3, g4])
        run_pipeline(chains, 10)

    def phase_A2(sb):
        Wo = sb("a2wo", [128, 8, 1024], BF16)
        dWo = Dep()
        for oc in range(8):
            dma("gpsimd", Wo[:, oc, :], gw_out[oc], [], [dWo])
        onw = sb("a2onw", [128, 1], F32)
        donw = Dep()
        dma("sync", onw[:, :], gonorm_d[:, :], [], [donw])
        H8 = range(8)

        def per_head(name, shape, dt):
            return [sb(f"{name}{h}", shape, dt) for h in H8], [Dep() for _ in H8]
        Sf, dSf = per_head("a2Sf", [128, 128], F32)
        for h in H8:
            mset("gpsimd", Sf[h][:, :], 0.0, [dSf[h]])
        qt = [sb(f"a2q{s}", [128, 8, 128], F32) for s in range(2)]
        kt = [sb(f"a2k{s}", [128, 8, 128], F32) for s in range(2)]
        ktk = [sb(f"a2ktk{s}", [128, 8, 128], F32) for s in range(2)]
        vtk = [sb(f"a2vtk{s}", [128, 8, 128], F32) for s in range(2)]
        szt = [sb(f"a2sz{s}", [128, 8, 128], F32) for s in range(2)]
        dld = [Dep() for _ in range(2)]
        junk = sb("a2junk", [128, 128], F32)
        djunk = Dep()
        ogt = sb("a2og", [128, 8, 512], BF16)
        dog = [Dep() for _ in H8]
        ngb = sb("a2ngb", [128, 8, 128], F32)
        dngb = [Dep() for _ in H8]
        t1, dt1 = per_head("a2t1", [128, 128], F32)
        dec, ddec = t1, dt1
        decs, ddecs = per_head("a2decs", [128, 128], F32)
        aqk, daqk = per_head("a2aqk", [128, 128], F32)
        Xa, dXa = per_head("a2Xa", [128, 128], F32)
        Xb, dXb = per_head("a2Xb", [128, 128], F32)
        XTa, dXTa = per_head("a2XTa", [128, 128], F32)
        XTb, dXTb = per_head("a2XTb", [128, 128], F32)
        Ra, dRa = per_head("a2Ra", [128, 128], F32)
        Rb, dRb = per_head("a2Rb", [128, 128], F32)
        kb, dkb = per_head("a2kb", [128, 128], F32)
        kd, dkd = per_head("a2kd", [128, 128], F32)
        wT, dwT = per_head("a2wT", [128, 128], F32)
        ub, dub = per_head("a2ub", [128, 128], F32)
        vn, dvn = per_head("a2vn", [128, 128], F32)
        o1, do1 = per_head("a2o1", [128, 128], F32)
        osb, dosb = o1, do1
        onb, donb = per_head("a2onb", [128, 128], F32)
        st2, dst2 = per_head("a2st", [128, 4], F32)
        ys = [sb(f"a2y{c}", [128, 256], F32) for c in range(8)]
        dys = [Dep() for _ in range(8)]
        sq = [sb(f"a2sq{c}", [128, 256], BF16) for c in range(8)]
        dsq = [Dep() for _ in range(8)]
        sdt = sb("a2sdt", [128, 256], F32)
        rstd = sb("a2rstd", [128, 256], F32)
        dst_ = Dep()
        tmp, dtmp = ys, dys
        xr = [sb(f"a2xr{c}", [128, 256], F32) for c in range(8)]
        dxr = [Dep() for _ in range(8)]

        def loads(ti):
            s = ti % 2
            tc = slice(ti * 128, (ti + 1) * 128)
            dma("sync", qt[s][:, :, :], qT_s[:, :, tc].rearrange("h d t -> d h t"), [], [dld[s]])
            dma("sync", kt[s][:, :, :], kT_s[:, :, tc].rearrange("h d t -> d h t"), [], [dld[s]])
            dma("sync", ktk[s][:, :, :].rearrange("p h d -> p (h d)"), ktokf_s[ti].rearrange("p h d -> p (h d)"), [], [dld[s]])
            dma("sync", vtk[s][:, :, :].rearrange("p h d -> p (h d)"), vtokf_s[ti].rearrange("p h d -> p (h d)"), [], [dld[s]])
            dma("sync", szt[s][:, :, :], sz_s[:, :, tc].rearrange("h e t -> e h t"), [], [dld[s]])

        loads(0)
        for ti in range(32):
            j, i = ti // 4, ti % 4
            s = ti % 2
            c0 = j * 512
            tsl = slice(i * 128, (i + 1) * 128)
            L = [dld[s]]
            if ti + 1 < 32:
                loads(ti + 1)
            B = [banks[h] for h in H8]
            bd = [bdep[h] for h in H8]
            for h in H8:
                cp("gpsimd", ngb[:, h, :], ng_all[:, ti, h:h + 1].to_broadcast([128, 128]), [dsmall], [dngb[h]])
            for h in H8:
                mm(B[h][:, 0:128], g_all[:, ti, h:h + 1].to_broadcast([128, 128]), ltri, True, False, [dsmall, dcst], [bd[h]])
                mm(B[h][:, 0:128], ltri, ngb[:, h, :], False, True, [dngb[h], dcst], [bd[h]])
                mm(B[h][:, 128:256], kt[s][:, h, :], qt[s][:, h, :], True, True, L, [bd[h]])
                mm(B[h][:, 256:384], kt[s][:, h, :], kt[s][:, h, :], True, True, L, [bd[h]])
            for h in H8:
                vts("gpsimd", kb[h][:, :], ktk[s][:, h, :], eg_all[:, ti, h:h + 1], ALU.mult, L + [dsmall], [dkb[h]])
                vts("gpsimd", kd[h][:, :], ktk[s][:, h, :], eg_all[:, ti, 8 + h:9 + h], ALU.mult, L + [dsmall], [dkd[h]])
            for h in H8:
                stt("vector", t1[h][:, :], B[h][:, 0:128], 0.0, negmask, ALU.min, ALU.add, [dcst], [bd[h], dt1[h]])
            for h in H8:
                act(dec[h][:, :], t1[h][:, :], AF.Exp, [dt1[h]], [ddec[h]])
            for h in H8:
                tt("gpsimd", decs[h][:, :], dec[h][:, :], strict01, ALU.mult, [ddec[h], dcst], [ddecs[h]])
            for h in H8:
                tt("vector", aqk[h][:, :], B[h][:, 128:256], dec[h][:, :], ALU.mult, [ddec[h]], [bd[h], daqk[h]])
            for h in H8:
                stt("vector", Xa[h][:, :], B[h][:, 256:384], nbeta_all[:, ti, h:h + 1], decs[h][:, :], ALU.mult, ALU.mult, [ddecs[h], dsmall], [bd[h], dXa[h]])
            for h in H8:
                tt("gpsimd", Ra[h][:, :], Xa[h][:, :], ident_f, ALU.add, [dXa[h], dcst], [dRa[h]])
            for h in H8:
                tr(B[h][:, 0:128], Xa[h][:, :], ident_f, [dXa[h], dcst], [bd[h]])
            for h in H8:
                cp("scalar", XTa[h][:, :], B[h][:, 0:128], [], [bd[h], dXTa[h]])
            Xc, dXc, Xn, dXn = Xa, dXa, Xb, dXb
            XTc, dXTc, XTn, dXTn = XTa, dXTa, XTb, dXTb
            Rc, dRc, Rn, dRn = Ra, dRa, Rb, dRb
            for lev in range(1, 7):
                for h in H8:
                    mm(B[h][:, 0:128], Xc[h][:, :], XTc[h][:, :], True, True, [dXc[h], dXTc[h]], [bd[h]])
                    if lev < 6:
                        mm(B[h][:, 128:256], XTc[h][:, :], Xc[h][:, :], True, True, [dXc[h], dXTc[h]], [bd[h]])
                for h in H8:
                    cp("scalar", XTn[h][:, :], B[h][:, 0:128], [], [bd[h], dXTn[h]])
                if lev < 6:
                    for h in H8:
                        cp("vector", Xn[h][:, :], B[h][:, 128:256], [], [bd[h], dXn[h]])
                for h in H8:
                    mm(B[h][:, 256:384], XTn[h][:, :], Rc[h][:, :], True, True, [dXTn[h], dRc[h]], [bd[h]])
                for h in H8:
                    tt("vector", Rn[h][:, :], B[h][:, 256:384], Rc[h][:, :], ALU.add, [dRc[h]], [bd[h], dRn[h]])
                Xc, dXc, Xn, dXn = Xn, dXn, Xc, dXc
                XTc, dXTc, XTn, dXTn = XTn, dXTn, XTc, dXTc
                Rc, dRc, Rn, dRn = Rn, dRn, Rc, dRc
            for h in H8:
                mm(B[h][:, 0:128], kb[h][:, :], Rc[h][:, :], True, True, [dkb[h], dRc[h]], [bd[h]])
                mm(B[h][:, 128:256], Rc[h][:, :], vtk[s][:, h, :], True, True, L + [dRc[h]], [bd[h]])
            for h in H8:
                cp("scalar", wT[h][:, :], B[h][:, 0:128], [], [bd[h], dwT[h]])
            for h in H8:
                vts("vector", ub[h][:, :], B[h][:, 128:256], beta_all[:, ti, h:h + 1], ALU.mult, [dsmall], [bd[h], dub[h]])
            for h in H8:
                mm(B[h][:, 256:384], wT[h][:, :], Sf[h][:, :], True, True, [dwT[h], dSf[h]], [bd[h]])
                mm(B[h][:, 384:512], qt[s][:, h, :], Sf[h][:, :], True, True, L + [dSf[h]], [bd[h]])
            for h in H8:
                stt("vector", vn[h][:, :], B[h][:, 256:384], nbeta_all[:, ti, h:h + 1], ub[h][:, :], ALU.mult, ALU.add, [dub[h], dsmall], [bd[h], dvn[h]])
            for h in H8:
                vts("vector", o1[h][:, :], B[h][:, 384:512], eg_all[:, ti, h:h + 1], ALU.mult, [dsmall], [bd[h], do1[h]])
            for h in H8:
                mm(B[h][:, 0:128], aqk[h][:, :], vn[h][:, :], True, True, [daqk[h], dvn[h]], [bd[h]])
                mm(B[h][:, 128:256], kd[h][:, :], vn[h][:, :], True, True, [dkd[h], dvn[h]], [bd[h]])
            for h in H8:
                stt("vector", Sf[h][:, :], Sf[h][:, :], eg_all[:, ti, 16 + h:17 + h], B[h][:, 128:256], ALU.mult, ALU.add, [dsmall], [bd[h], dSf[h]])
            for h in H8:
                tt("vector", osb[h][:, :], B[h][:, 0:128], o1[h][:, :], ALU.add, [do1[h]], [bd[h], dosb[h]])
            for h in H8:
                mset("gpsimd", st2[h][:, 0:1], 0.0, [dst2[h]])
            for h in H8:
                act(junk[:, :], osb[h][:, :], AF.Square, [dosb[h]], [djunk, dst2[h]], accum=st2[h][:, 0:1])
            for h in H8:
                act(st2[h][:, 1:2], st2[h][:, 0:1], AF.Sqrt, [dst2[h]], [dst2[h]], bias=EPS, scale=1.0 / 128)
            for h in H8:
                recip(st2[h][:, 2:3], st2[h][:, 1:2], [dst2[h]], [dst2[h]])
            for h in H8:
                vts("gpsimd", onb[h][:, :], osb[h][:, :], st2[h][:, 2:3], ALU.mult, [dosb[h], dst2[h]], [donb[h]])
            for h in H8:
                tr(B[h][:, 256:384], onb[h][:, :], ident_f, [donb[h], dcst], [bd[h]])
            for h in H8:
                stt("vector", ogt[:, h, tsl], B[h][:, 256:384], onw[:, 0:1], szt[s][:, h, :], ALU.mult, ALU.mult, L + [donw], [bd[h], dog[h]])
            if i == 3:
                for hf in range(2):
                    cc = c0 + hf * 256
                    for c in range(8):
                        dma("sync", xr[c][:, :], xT[c * 128:(c + 1) * 128, cc:cc + 256], [], [dxr[c]])
                    for oc in range(8):
                        b = nb()
                        for kc in range(8):
                            mm(banks[b][:, 0:256], Wo[:, oc, kc * 128:(kc + 1) * 128], ogt[:, kc, hf * 256:(hf + 1) * 256], kc == 0, kc == 7, [dWo, dog[kc]], [bdep[b]])
                        cp("scalar", ys[oc][:, :], banks[b][:, 0:256], [], [bdep[b], dys[oc]])
                    post_norm_residual(ys, dys, 256, 2, xr, dxr, R1, cc, sq, dsq, sdt, rstd, dst_, tmp, dtmp)

    def make_ffn(layer, Rin, Rout, pre=None):
        TW = 256

        def phase(sb):
            if pre is None:
                Wi = sb("fwi", [128, 44, 1024], BF16)
                dWi = [Dep() for _ in range(44)]
                Wo = sb("fwo", [128, 8, DFF], BF16)
                dWo = [Dep() for _ in range(8)]
                for fc in range(22):
                    for oc in (fc, 22 + fc):
                        dma("gpsimd", Wi[:, oc, :], wffn_in[layer, oc], [], [dWi[oc]])
                for oc in range(8):
                    dma("gpsimd", Wo[:, oc, :], wffn_out[layer, oc], [], [dWo[oc]])
            else:
                Wi, dWi, Wo, dWo, pending = pre
                while pending:
                    pending.pop(0)()
            xs = [[sb(f"fx{s}_{c}", [128, TW], F32) for c in range(8)] for s in range(2)]
            dxs = [[Dep() for c in range(8)] for s in range(2)]
            hb2 = [[sb(f"fhb{s}_{c}", [128, TW], BF16) for c in range(8)] for s in range(2)]
            dhb2 = [[Dep() for _ in range(8)] for s in range(2)]
            sq = [sb(f"fsq{c}", [128, TW], BF16) for c in range(8)]
            dsq = [Dep() for _ in range(8)]
            sdt = sb("fsdt", [128, TW], F32)
            rstd = sb("frstd", [128, TW], F32)
            dst_ = Dep()
            sqi = [sb(f"fsqi{c}", [128, TW], BF16) for c in range(8)]
            dsqi = [Dep() for _ in range(8)]
            sdti = sb("fsdti", [128, TW], F32)
            rstdi = sb("frstdi", [128, TW], F32)
            dsti = Dep()
            sg = [sb(f"fsg{i}", [128, TW], F32) for i in range(2)]
            dsg = [Dep() for _ in range(2)]
            at = [sb(f"fa{f}", [128, TW], BF16) for f in range(22)]
            dat = [Dep() for _ in range(22)]
            ys = [sb(f"fy{c}", [128, TW], F32) for c in range(8)]
            dys = [Dep() for _ in range(8)]
            tmp, dtmp = ys, dys
            def norm_in(j):
                s = j % 2
                c0 = j * TW
                for c in range(8):
                    dma("sync", xs[s][c][:, :], Rin[c * 128:(c + 1) * 128, c0:c0 + TW], [], [dxs[s][c]])
                r = rms_rstd([x[:, :] for x in xs[s]], dxs[s], TW, sqi, dsqi, sdti, rstdi, dsti, 1.0 / D)
                for c in range(8):
                    stt("vector", hb2[s][c][:, :], xs[s][c][:, :], nw(4 + layer, c), r, ALU.mult, ALU.mult, [dxs[s][c], dsti, dcst], [dhb2[s][c]])

            NTL = T // TW
            norm_in(0)
            for j in range(NTL):
                s = j % 2
                c0 = j * TW
                hb, dhb = hb2[s], dhb2[s]
                for fc in range(22):
                    bg = nb()
                    for kc in range(8):
                        mm(banks[bg][:, :TW], Wi[:, fc, kc * 128:(kc + 1) * 128], hb[kc][:, :], kc == 0, kc == 7, [dWi[fc], dhb[kc]], [bdep[bg]])
                    bu = nb()
                    for kc in range(8):
                        mm(banks[bu][:, :TW], Wi[:, 22 + fc, kc * 128:(kc + 1) * 128], hb[kc][:, :], kc == 0, kc == 7, [dWi[22 + fc], dhb[kc]], [bdep[bu]])
                    u = fc % 2
                    act(sg[u][:, :], banks[bg][:, :TW], AF.Silu, [], [bdep[bg], dsg[u]])
                    tt("vector", at[fc][:, :], banks[bu][:, :TW], sg[u][:, :], ALU.mult, [dsg[u]], [bdep[bu], dat[fc]])
                if j + 1 < NTL:
                    norm_in(j + 1)
                for oc in range(8):
                    b = nb()
                    for fc in range(22):
                        mm(banks[b][:, :TW], Wo[:, oc, fc * 128:(fc + 1) * 128], at[fc][:, :], fc == 0, fc == 21, [dWo[oc], dat[fc]], [bdep[b]])
                    cp("scalar", ys[oc][:, :], banks[b][:, :TW], [], [bdep[b], dys[oc]])
                post_norm_residual(ys, dys, TW, 6 + layer, xs[s], dxs[s], Rout, c0, sq, dsq, sdt, rstd, dst_, tmp, dtmp)
        return phase

    def phase_B01(sb):
        Wk = sb("bwk", [128, 8, 1024], BF16)
        Wq = sb("bwq", [128, 8, 1024], BF16)
        Wv = sb("bwv", [128, 8, 1024], BF16)
        Wf = sb("bwf", [128, 8, 16], BF16)
        dW = Dep()
        for oc in range(8):
            dma("gpsimd", Wk[:, oc, :], wk_d[oc], [], [dW])
            dma("gpsimd", Wq[:, oc, :], wq_d[oc], [], [dW])
        dma("gpsimd", Wv[:, :, :], wv_d[:, :, :], [], [dW])
        dma("gpsimd", Wf[:, :, :], wf_d[:, :, :], [], [dW])
        bfor = sb("bbfor", [128, 16], F32)
        dma("sync", bfor[:, :], bfor_d[:, :], [], [dW])
        ones3 = sb("bones3", [128, T], BF16)
        done3 = Dep()
        mset("gpsimd", ones3[:, :], 1.0, [done3])
        for h in range(16):
            dma("sync", kx_s[h, 64:67, :], ones3[0:3, :], [done3], [Dep()])
        carry = sb("bcarry", [128, 16], F32)
        dcarry = Dep()
        mset("gpsimd", carry[:, :], 0.0, [dcarry])
        xs = [[sb(f"bx{s}_{c}", [128, 512], F32) for c in range(8)] for s in range(2)]
        dxs = [[Dep() for c in range(8)] for s in range(2)]
        hk2 = [[sb(f"bhk{s}_{c}", [128, 512], BF16) for c in range(8)] for s in range(2)]
        dhk2 = [[Dep() for _ in range(8)] for s in range(2)]
        hq2 = [[sb(f"bhq{s}_{c}", [128, 512], BF16) for c in range(8)] for s in range(2)]
        dhq2 = [[Dep() for _ in range(8)] for s in range(2)]
        sq = [sb(f"bsq{c}", [128, 512], BF16) for c in range(8)]
        dsq = [Dep() for _ in range(8)]
        sdt = sb("bsdt", [128, 512], F32)
        rstd = sb("brstd", [128, 512], F32)
        dst_ = Dep()
        ob = [sb(f"bob{i}", [128, 512], BF16) for i in range(2)]
        dob = [Dep() for _ in range(2)]
        vx = [sb(f"bvx{i}", [128, 16, 65], BF16) for i in range(2)]
        dvx = [Dep() for _ in range(2)]
        for i in range(2):
            mset("gpsimd", vx[i][:, :, :], 1.0, [dvx[i]])
        tl = sb("btl", [128, 4, 16], F32)
        dtl = Dep()
        crow = sb("bcrow", [16, 512], F32)
        r1 = sb("br1", [16, 512], F32)
        chi = sb("bchi", [16, 3, 512], BF16)
        dcr = Dep()
        cnt = [0]
        def norm_in(j):
            s = j % 2
            c0 = j * 512
            for c in range(8):
                dma("sync", xs[s][c][:, :], R2[c * 128:(c + 1) * 128, c0:c0 + 512], [], [dxs[s][c]])
            r = rms_rstd([x[:, :] for x in xs[s]], dxs[s], 512, sq, dsq, sdt, rstd, dst_, 1.0 / D)
            for c in range(8):
                stt("vector", hk2[s][c][:, :], xs[s][c][:, :], nw(8, c), r, ALU.mult, ALU.mult, [dxs[s][c], dst_, dcst], [dhk2[s][c]])
                stt("vector", hq2[s][c][:, :], xs[s][c][:, :], nw(1, c), r, ALU.mult, ALU.mult, [dxs[s][c], dst_, dcst], [dhq2[s][c]])

        norm_in(0)
        for j in range(8):
            s = j % 2
            c0 = j * 512
            hk, dhk, hq, dhq = hk2[s], dhk2[s], hq2[s], dhq2[s]
            for (Wm, hsrc, dh, dstd, scl) in ((Wk, hk, dhk, kx_s, 1.0), (Wq, hq, dhq, qx_s, 0.125)):
                for oc in range(8):
                    b = nb()
                    for kc in range(8):
                        mm(banks[b][:, :], Wm[:, oc, kc * 128:(kc + 1) * 128], hsrc[kc][:, :], kc == 0, kc == 7, [dW, dh[kc]], [bdep[b]])
                    u = cnt[0] % 2
                    cnt[0] += 1
                    vts("vector", ob[u][:, :], banks[b][:, :], scl, ALU.mult, [], [bdep[b], dob[u]])
                    dma("sync", dstd[2 * oc, 0:64, c0:c0 + 512], ob[u][0:64, :], [dob[u]], [Dep()])
                    dma("sync", dstd[2 * oc + 1, 0:64, c0:c0 + 512], ob[u][64:128, :], [dob[u]], [Dep()])
            if j + 1 < 8:
                norm_in(j + 1)
            for i in range(4):
                ti = 4 * j + i
                tsl = slice(i * 128, (i + 1) * 128)
                u = ti % 2
                for half in range(2):
                    b = nb()
                    for kc in range(8):
                        mm(banks[b][:, :], hk[kc][:, tsl], Wv[:, kc, half * 512:(half + 1) * 512], kc == 0, kc == 7, [dW, dhk[kc]], [bdep[b]])
                    if half == 0:
                        cp("vector", vx[u][:, 0:8, 0:64], banks[b][:, :].rearrange("p (h e) -> p h e", h=8), [], [bdep[b], dvx[u]])
                    else:
                        cp("scalar", vx[u][:, 8:16, 0:64], banks[b][:, :].rearrange("p (h e) -> p h e", h=8), [], [bdep[b], dvx[u]])
                dma("sync", vx_s[ti].rearrange("p h e -> p (h e)"), vx[u][:, :, :].rearrange("p h e -> p (h e)"), [dvx[u]], [Dep()])
                b = nb()
                for kc in range(8):
                    mm(banks[b][:, 0:16], hk[kc][:, tsl], Wf[:, kc, :], kc == 0, kc == 7, [dW, dhk[kc]], [bdep[b]])
                tt("vector", tl[:, i, :], banks[b][:, 0:16], bfor[:, :], ALU.add, [dW], [bdep[b], dtl])
                act(tl[:, i, :], tl[:, i, :], AF.Exp, [dtl], [dtl], scale=-1.0)
                act(tl[:, i, :], tl[:, i, :], AF.Ln, [dtl], [dtl], bias=1.0)
                b2 = nb()
                mm(banks[b2][:, 0:16], ltri, tl[:, i, :], True, True, [dtl, dcst], [bdep[b2]])
                mm(banks[b2][:, 16:32], ones_f, tl[:, i, :], True, True, [dtl, dcst], [bdep[b2]])
                tt("vector", negc_all[:, ti, :], banks[b2][:, 0:16], carry[:, :], ALU.add, [dcarry], [bdep[b2], dsmall])
                tt("vector", carry[:, :], banks[b2][:, 16:32], carry[:, :], ALU.add, [], [bdep[b2], dcarry])
                b3 = nb()
                tr(banks[b3][0:16, 0:128], negc_all[:, ti, :], ident_f, [dsmall, dcst], [bdep[b3]])
                vts("vector", crow[:, tsl], banks[b3][0:16, 0:128], -1.0, ALU.mult, [], [bdep[b3], dcr])
            cp("vector", chi[:, 0, :], crow[:, :], [dcr], [dcr])
            tt("vector", r1[:, :], crow[:, :], chi[:, 0, :], ALU.subtract, [dcr], [dcr])
            cp("vector", chi[:, 1, :], r1[:, :], [dcr], [dcr])
            tt("vector", r1[:, :], r1[:, :], chi[:, 1, :], ALU.subtract, [dcr], [dcr])
            cp("vector", chi[:, 2, :], r1[:, :], [dcr], [dcr])
            for q3 in range(3):
                dma("sync", qx_s[:, 64 + q3, c0:c0 + 512], chi[:, q3, :], [dcr], [Dep()])

    def phase_B2(sb):
        kx = [sb(f"ckx{i}", [128, T], BF16) for i in range(2)]
        qx = [sb(f"cqx{i}", [128, T], BF16) for i in range(2)]
        vx = [sb(f"cvx{i}", [128, 32, 65], BF16) for i in range(2)]
        dld = [Dep() for _ in range(2)]
        NPT = 8
        LA = 4
        pt = [sb(f"cpt{i}", [128, 512], BF16) for i in range(NPT)]
        dpt = [Dep() for _ in range(NPT)]
        rlt = [sb(f"crl{i}", [128, 512], F32) for i in range(2)]
        drl = [Dep() for _ in range(2)]
        bc = sb("cbc", [64, 512], F32)
        dbc = Dep()
        ot = [sb(f"cot{i}", [64, 512], BF16) for i in range(2)]
        dot = [Dep() for _ in range(2)]
        ACC = [0, 1]
        SBK = [2, 3, 4, 5, 7]
        NBK = 6
        blocks = []
        for h in range(16):
            for jq in range(8):
                for ik in range(4 * jq + 4):
                    blocks.append((h, jq, ik))
        n = len(blocks)

        def head_loads(h):
            s = h % 2
            dma("sync", kx[s][0:67, :], kx_s[h, :, :], [], [dld[s]])
            dma("sync", qx[s][0:67, :], qx_s[h, :, :], [], [dld[s]])
            for g4 in range(8):
                dma("sync", vx[s][:, g4 * 4:(g4 + 1) * 4, :], vx_s[g4 * 4:(g4 + 1) * 4, :, h, :].rearrange("t p e -> p t e"), [], [dld[s]])

        def s_side(idx):
            h, jq, ik = blocks[idx]
            s = h % 2
            if idx % 40 == 20 and f1_pending:
                f1_pending.pop(0)()
            if h == 0 and jq == 0 and ik == 0:
                head_loads(0)
            if jq == 5 and ik == 0 and h + 1 < 16:
                head_loads(h + 1)
            diag = ik - 4 * jq
            t0 = max(0, diag) * 128
            bs = SBK[idx % 5]
            u = idx % NPT
            mm(banks[bs][:, t0:512], kx[s][0:67, ik * 128:(ik + 1) * 128], qx[s][0:67, jq * 512 + t0:(jq + 1) * 512], True, True, [dld[s]], [bdep[bs]])
            act(pt[u][:, t0:512], banks[bs][:, t0:512], AF.Exp, [dsmall], [bdep[bs], dpt[u]], bias=negc_all[:, ik, h:h + 1])
            if diag >= 0:
                tt("gpsimd", pt[u][:, t0:t0 + 128], pt[u][:, t0:t0 + 128], incl01_b, ALU.mult, [dcst], [dpt[u]])

        def pv_side(idx):
            h, jq, ik = blocks[idx]
            s = h % 2
            nk = 4 * jq + 4
            diag = ik - 4 * jq
            t0 = max(0, diag) * 128
            u = idx % NPT
            ba = ACC[jq % 2]
            mm(banks[ba][0:65, t0:512], vx[s][:, ik, :], pt[u][:, t0:512], ik == 0, ik == nk - 1, [dld[s], dpt[u]], [bdep[ba]])
            if ik == nk - 1:
                r2 = jq % 2
                cp("vector", rlt[r2][0:65, :], banks[ba][0:65, :], [], [bdep[ba], drl[r2]])
                mm(banks[NBK][0:64, :], ident_f[0:65, 64:65].to_broadcast([65, 64]), rlt[r2][0:65, :], True, True, [drl[r2], dcst], [bdep[NBK]])
                recip(bc[:, :], banks[NBK][0:64, :], [], [bdep[NBK], dbc])
                tt("vector", ot[r2][:, :], rlt[r2][0:64, :], bc[:, :], ALU.mult, [dbc, drl[r2]], [dot[r2]])
                dma("sync", at_s[h * 64:(h + 1) * 64, jq * 512:(jq + 1) * 512], ot[r2][:, :], [dot[r2]], [Dep()])

        for idx in range(n + LA):
            if idx < n:
                s_side(idx)
            if idx >= LA:
                pv_side(idx - LA)

    def phase_B3(sb):
        Wo = sb("dwo", [128, 8, 1024], BF16)
        dWo = Dep()
        for oc in range(8):
            dma("gpsimd", Wo[:, oc, :], wo_d[oc], [], [dWo])
        WB = 256
        at = [[sb(f"dat{s}_{c}", [128, WB], BF16) for c in range(8)] for s in range(2)]
        dat = [[Dep() for c in range(8)] for s in range(2)]
        xr = [sb(f"dxr{c}", [128, WB], F32) for c in range(8)]
        dxr = [Dep() for _ in range(8)]
        ys = [sb(f"dy{c}", [128, WB], F32) for c in range(8)]
        dys = [Dep() for _ in range(8)]
        sq = [sb(f"dsq{c}", [128, WB], BF16) for c in range(8)]
        dsq = [Dep() for _ in range(8)]
        sdt = sb("dsdt", [128, WB], F32)
        rstd = sb("drstd", [128, WB], F32)
        dst_ = Dep()
        tmp, dtmp = ys, dys
        for j in range(T // WB):
            s = j % 2
            c0 = j * WB
            for c in range(8):
                dma("sync", at[s][c][:, :], at_s[c * 128:(c + 1) * 128, c0:c0 + WB], [], [dat[s][c]])
                dma("sync", xr[c][:, :], R2[c * 128:(c + 1) * 128, c0:c0 + WB], [], [dxr[c]])
            for oc in range(8):
                b = nb()
                for kc in range(8):
                    mm(banks[b][:, :WB], Wo[:, oc, kc * 128:(kc + 1) * 128], at[s][kc][:, :], kc == 0, kc == 7, [dWo, dat[s][kc]], [bdep[b]])
                cp("scalar", ys[oc][:, :], banks[b][:, :WB], [], [bdep[b], dys[oc]])
            post_norm_residual(ys, dys, WB, 3, xr, dxr, R3, c0, sq, dsq, sdt, rstd, dst_, tmp, dtmp)

    f1_pending = []
    f1_pre = [None]

    def alloc_f1_weights():
        Wi = gsb("f1wi", [128, 44, 1024], BF16)
        Wo = gsb("f1wo", [128, 8, DFF], BF16)
        dWi = [Dep() for _ in range(44)]
        dWo = [Dep() for _ in range(8)]

        def mk(dst, src, dep):
            return lambda: dma("gpsimd", dst, src, [], [dep])
        for fc in range(22):
            for oc in (fc, 22 + fc):
                f1_pending.append(mk(Wi[:, oc, :], wffn_in[1, oc], dWi[oc]))
        for oc in range(8):
            f1_pending.append(mk(Wo[:, oc, :], wffn_out[1, oc], dWo[oc]))
        f1_pre[0] = (Wi, dWi, Wo, dWo, f1_pending)

    phases = {
        "A1": phase_A1,
        "A2": phase_A2,
        "F0": make_ffn(0, R1, R2),
        "B01": phase_B01,
        "B2": phase_B2,
        "B3": phase_B3,
    }
    for name in PHASES:
        if name == "B2":
            alloc_f1_weights()
        if name == "F1":
            phases["F1"] = make_ffn(1, R3, outT, pre=f1_pre[0])
        run_phase(phases[name])
        if stop_after == name:
            break
    print(f"[kernel] built program: {P.nops} ops, {len(P.sems)} semaphores")
    return nc


def _oc_tiles(w):
    K, N = w.shape
    return np.ascontiguousarray(w.reshape(K // 128, 128, N // 128, 128).transpose(2, 1, 0, 3).reshape(N // 128, 128, (K // 128) * 128))


def _mov_tiles(w):
    K, N = w.shape
    return np.ascontiguousarray(w.reshape(K // 128, 128, N).transpose(1, 0, 2))


def _consts():
    s = np.arange(128)[:, None]
    c = np.arange(128)[None, :]
    ident = (s == c).astype(np.float32)
    ltri = (s <= c).astype(np.float32)
    ustr = (s > c).astype(np.float32)
    ones = np.ones((128, 128), np.float32)
    negmask = np.where(c >= s, 0.0, -30000.0).astype(np.float32)
    strict01 = (c > s).astype(np.float32)
    incl01 = (c >= s).astype(np.float32)
    return np.ascontiguousarray(np.stack([ident, ltri, ustr, ones, negmask, strict01, incl01], axis=1))


def prepare_inputs(x, pre_mix_norm, post_mix_norm, pre_ffn_norm, post_ffn_norm, w_ffn_in, w_ffn_out,
                   gdn_w_in, gdn_conv, gdn_a_log, gdn_dt_bias, gdn_out_norm, gdn_w_out,
                   kv_norm, w_kv, b_forget, fox_w_q, fox_w_o):
    f = lambda a: np.asarray(a, dtype=np.float32)
    x = f(x)
    vecs = [f(pre_mix_norm)[0], f(pre_mix_norm)[1], f(post_mix_norm)[0], f(post_mix_norm)[1],
            f(pre_ffn_norm)[0], f(pre_ffn_norm)[1], f(post_ffn_norm)[0], f(post_ffn_norm)[1], f(kv_norm)]
    nrm = np.ascontiguousarray(np.concatenate([v.reshape(8, 128).T for v in vecs], axis=1))
    gwin = f(gdn_w_in)[0]
    shared = {
        "nrm": nrm,
        "gw_in": _oc_tiles(gwin[:, :4096]),
        "gw_tail": _mov_tiles(gwin[:, 4096:4112]),
        "gconv": np.ascontiguousarray(f(gdn_conv)[0].reshape(4, 24, 128).transpose(2, 1, 0)),
        "galog": np.ascontiguousarray(np.broadcast_to(f(gdn_a_log)[0][None, :], (128, 8))),
        "gdtb": np.ascontiguousarray(np.broadcast_to(f(gdn_dt_bias)[0][None, :], (128, 8))),
        "gonorm": np.ascontiguousarray(f(gdn_out_norm)[0].reshape(128, 1)),
        "gw_out": _oc_tiles(f(gdn_w_out)[0]),
        "wffn_in": np.stack([_oc_tiles(f(w_ffn_in)[l]) for l in range(2)]),
        "wffn_out": np.stack([_oc_tiles(f(w_ffn_out)[l]) for l in range(2)]),
        "wk": _oc_tiles(f(w_kv)[:, :1024]),
        "wv": _mov_tiles(f(w_kv)[:, 1024:2048]),
        "wf": _mov_tiles(f(w_kv)[:, 2048:2064]),
        "bfor": np.ascontiguousarray(np.broadcast_to(f(b_forget)[None, :], (128, 16))),
        "wq": _oc_tiles(f(fox_w_q)[0]),
        "wo": _oc_tiles(f(fox_w_o)[0]),
        "cst": _consts(),
    }
    in_maps = []
    for b in range(NCORES):
        m = dict(shared)
        m["xT"] = np.ascontiguousarray(x[b].T)
        in_maps.append(m)
    return in_maps


def kernel(**inputs):
    in_maps = prepare_inputs(**inputs)
    nc = build_program()
    res = run_bass_kernel_spmd(nc, in_maps, core_ids=list(range(NCORES)))
    out = np.stack([np.ascontiguousarray(r["outT"].T) for r in res.results], axis=0)
    return out.astype(np.float32)
```

```python
import contextlib
import numpy as np
import concourse.bass as bass
import concourse.mybir as mybir
from concourse.bass_utils import run_bass_kernel_spmd

F32 = mybir.dt.float32
BF16 = mybir.dt.bfloat16
AF = mybir.ActivationFunctionType
ALU = mybir.AluOpType

SAME_ENGINE_SYNC = True
SEM_ROTATE = 12000
DMA_SEMS_PER_Q = 8

T = 4096
D = 1024
NCORES = 8
DFF = 2816
EPS = 1e-6


class Dep:
    __slots__ = ("w", "r")

    def __init__(self):
        self.w = None
        self.r = []


class _Eng:
    def __init__(self, name):
        self.name = name
        self.ops = []
        self.clock = {}
        self.sem = None
        self.count = 0
        self.dma_sems = []
        self.dma_counts = []
        self.dma_last = []
        self.dma_i = 0


class Prog:
    ENGS = ("tensor", "vector", "scalar", "gpsimd", "sync")

    def __init__(self, nc):
        self.nc = nc
        self.e = {n: _Eng(n) for n in self.ENGS}
        self.sems = []
        self.nsem = 0
        self.nops = 0
        for n in self.ENGS:
            self._new_sem(self.e[n])

    def _alloc_sem(self, name):
        s = self.nc.alloc_semaphore(name)
        self.sems.append(s)
        return len(self.sems) - 1

    def _new_sem(self, E):
        E.sem = self._alloc_sem(f"s_{E.name}_{self.nsem}")
        self.nsem += 1
        E.count = 0

    def _dma_slot(self, E):
        if len(E.dma_sems) < DMA_SEMS_PER_Q:
            E.dma_sems.append(self._alloc_sem(f"d_{E.name}_{len(E.dma_sems)}"))
            E.dma_counts.append(0)
            E.dma_last.append(None)
        i = E.dma_i % DMA_SEMS_PER_Q
        E.dma_i += 1
        return i

    def emit(self, eng, fn, reads=(), writes=(), dma=False):
        E = self.e[eng]
        need = {}

        def req(t):
            if t is None:
                return
            s, v, src, isdma = t
            if (not isdma) and src == eng:
                if eng == "tensor" or not SAME_ENGINE_SYNC:
                    return
            if need.get(s, 0) < v:
                need[s] = v

        for d in reads:
            req(d.w)
        for d in writes:
            req(d.w)
            for t in d.r:
                req(t)
        slot = None
        if dma:
            slot = self._dma_slot(E)
            req(E.dma_last[slot])
        waits = []
        for s, v in need.items():
            if E.clock.get(s, 0) < v:
                waits.append((s, v))
                E.clock[s] = v
        if dma:
            E.dma_counts[slot] += 16
            tk = (E.dma_sems[slot], E.dma_counts[slot], eng, True)
            E.dma_last[slot] = tk
            inc = 16
        else:
            if E.count >= SEM_ROTATE:
                self._new_sem(E)
            E.count += 1
            tk = (E.sem, E.count, eng, False)
            inc = 1
        E.ops.append((waits, fn, tk[0], inc))
        self.nops += 1
        for d in reads:
            d.r.append(tk)
        for d in writes:
            d.w = tk
            d.r = []
        return tk

    def barrier(self):
        last = []
        for E in self.e.values():
            if E.count > 0:
                last.append((E.sem, E.count))
            for t in E.dma_last:
                if t is not None:
                    last.append((t[0], t[1]))
        for F in self.e.values():
            waits = []
            for s, v in last:
                if F.name == "tensor" and s == F.sem:
                    continue
                if F.clock.get(s, 0) < v:
                    waits.append((s, v))
                    F.clock[s] = v
            F.ops.append((waits, None, None, 0))

    def replay(self):
        nc = self.nc
        sems = self.sems
        with nc.Block() as block:
            def mk(name):
                ops = self.e[name].ops

                def body(eng):
                    for waits, fn, s, inc in ops:
                        for ws, wv in waits:
                            eng.wait_ge(sems[ws], wv)
                        if fn is not None:
                            ins = fn(eng)
                            ins.then_inc(sems[s], inc)
                return body
            block.sync(mk("sync"))
            block.tensor(mk("tensor"))
            block.vector(mk("vector"))
            block.scalar(mk("scalar"))
            block.gpsimd(mk("gpsimd"))
        for E in self.e.values():
            E.ops = []


PHASES = ("A1", "A2", "F0", "B01", "B2", "B3", "F1")


def build_program(stop_after=None, debug=False):
    nc = bass.Bass("TRN2", target_bir_lowering=False)
    P = Prog(nc)

    def din(name, shape, dt=F32):
        return nc.dram_tensor(name, list(shape), dt, kind="ExternalInput").ap()

    def dscr(name, shape, dt):
        if debug:
            return nc.dram_tensor(name, list(shape), dt, kind="ExternalOutput").ap()
        return nc.dram_tensor(name, list(shape), dt).ap()

    xT = din("xT", [D, T])
    nrm_d = din("nrm", [128, 72])
    gw_in = din("gw_in", [32, 128, 1024])
    gw_tail = din("gw_tail", [128, 8, 16])
    gconv_d = din("gconv", [128, 24, 4])
    galog_d = din("galog", [128, 8])
    gdtb_d = din("gdtb", [128, 8])
    gonorm_d = din("gonorm", [128, 1])
    gw_out = din("gw_out", [8, 128, 1024])
    wffn_in = din("wffn_in", [2, 44, 128, 1024])
    wffn_out = din("wffn_out", [2, 8, 128, DFF])
    wk_d = din("wk", [8, 128, 1024])
    wv_d = din("wv", [128, 8, 1024])
    wf_d = din("wf", [128, 8, 16])
    bfor_d = din("bfor", [128, 16])
    wq_d = din("wq", [8, 128, 1024])
    wo_d = din("wo", [8, 128, 1024])
    cst_d = din("cst", [128, 7, 128])
    outT = nc.dram_tensor("outT", [D, T], F32, kind="ExternalOutput").ap()

    R1 = dscr("R1", [D, T], F32)
    R2 = dscr("R2", [D, T], F32)
    R3 = dscr("R3", [D, T], F32)
    qT_s = dscr("qT_s", [8, 128, T], F32)
    kT_s = dscr("kT_s", [8, 128, T], F32)
    ktokf_s = dscr("ktokf_s", [32, 128, 8, 128], F32)
    sz_s = dscr("sz_s", [8, 128, T], F32)
    vtokf_s = dscr("vtokf_s", [32, 128, 8, 128], F32)
    kx_s = dscr("kx_s", [16, 67, T], BF16)
    qx_s = dscr("qx_s", [16, 67, T], BF16)
    vx_s = dscr("vx_s", [32, 128, 16, 65], BF16)
    at_s = dscr("at_s", [D, T], BF16)

    banks = [nc.alloc_psum_tensor(f"pb{i}", [128, 512], F32) for i in range(8)]
    bankb = [b[:, :].bitcast(BF16) for b in banks]
    bdep = [Dep() for _ in range(8)]
    rr = [0]

    def nb():
        rr[0] = (rr[0] + 1) % 8
        return rr[0]

    def mm(out, lhsT, rhs, st, sp, r, w):
        P.emit("tensor", lambda e: e.matmul(out, lhsT=lhsT, rhs=rhs, start=st, stop=sp), r, w)

    def tr(out, in_, ident, r, w):
        P.emit("tensor", lambda e: e.transpose(out, in_, ident), r, w)

    def act(out, in_, func, r, w, bias=None, scale=None, accum=None):
        kw = {}
        if bias is not None:
            kw["bias"] = bias
        if scale is not None:
            kw["scale"] = scale
        if accum is not None:
            kw["accum_out"] = accum
        P.emit("scalar", lambda e: e.activation(out=out, in_=in_, func=func, **kw), r, w)

    def vts(eng, out, in0, s1, op0, r, w, s2=None, op1=None):
        if op1 is None:
            P.emit(eng, lambda e: e.tensor_scalar(out=out, in0=in0, scalar1=s1, scalar2=None, op0=op0), r, w)
        else:
            P.emit(eng, lambda e: e.tensor_scalar(out=out, in0=in0, scalar1=s1, scalar2=s2, op0=op0, op1=op1), r, w)

    def stt(eng, out, in0, scalar, in1, op0, op1, r, w):
        P.emit(eng, lambda e: e.scalar_tensor_tensor(out=out, in0=in0, scalar=scalar, in1=in1, op0=op0, op1=op1), r, w)

    def tt(eng, out, in0, in1, op, r, w):
        P.emit(eng, lambda e: e.tensor_tensor(out=out, in0=in0, in1=in1, op=op), r, w)

    def cp(eng, out, in_, r, w):
        if eng == "scalar":
            act(out, in_, AF.Copy, r, w)
        else:
            P.emit(eng, lambda e: e.tensor_copy(out=out, in_=in_), r, w)

    def recip(out, in_, r, w):
        P.emit("vector", lambda e: e.reciprocal(out=out, in_=in_), r, w)

    def mset(eng, ap, val, w):
        P.emit(eng, lambda e: e.memset(ap, val), [], w)

    def dma(q, out, in_, r, w):
        P.emit(q, lambda e: e.dma_start(out=out, in_=in_), r, w, dma=True)

    ddram = Dep()

    def gsb(name, shape, dt):
        return nc.alloc_sbuf_tensor("g_" + name, list(shape), dt)

    cst = gsb("cst", [128, 7, 128], F32)
    dcst = Dep()
    ident_f, ltri, ustr, ones_f, negmask, strict01, incl01 = [cst[:, i, :] for i in range(7)]
    cstb = gsb("cstb", [128, 3, 128], BF16)
    ident_b, ones_b, incl01_b = cstb[:, 0, :], cstb[:, 1, :], cstb[:, 2, :]
    nrm = gsb("nrm", [128, 72], F32)
    g_all = gsb("g_all", [128, 32, 8], F32)
    ng_all = gsb("ng_all", [128, 32, 8], F32)
    beta_all = gsb("beta_all", [128, 32, 8], F32)
    nbeta_all = gsb("nbeta_all", [128, 32, 8], F32)
    eg_all = gsb("eg_all", [128, 32, 24], F32)
    negc_all = gsb("negc_all", [128, 32, 16], F32)
    dsmall = Dep()

    dma("sync", cst[:, :, :], cst_d[:, :, :], [], [dcst])
    dma("sync", nrm[:, :], nrm_d[:, :], [], [dcst])
    cp("vector", cstb[:, 0, :], cst[:, 0, :], [dcst], [dcst])
    cp("vector", cstb[:, 1, :], cst[:, 3, :], [dcst], [dcst])
    cp("vector", cstb[:, 2, :], cst[:, 6, :], [dcst], [dcst])

    def nw(v, c):
        return nrm[:, v * 8 + c: v * 8 + c + 1]

    def rms_rstd(xs, dxs, W, sq, dsq, sdt, rstd, dst_, scale):
        n = len(xs)
        b = nb()
        for c in range(n):
            act(sq[c][:, :W], xs[c], AF.Square, [dxs[c]], [dsq[c]])
        for c in range(n):
            mm(banks[b][:, :W], ones_b, sq[c][:, :W], c == 0, c == n - 1, [dsq[c], dcst], [bdep[b]])
        act(sdt[:, :W], banks[b][:, :W], AF.Ln, [], [bdep[b], dst_], bias=EPS, scale=scale)
        act(rstd[:, :W], sdt[:, :W], AF.Exp, [dst_], [dst_], scale=-0.5)
        return rstd[:, :W]

    def post_norm_residual(ys, dys, W, nv, xres, dxres, res_out_dram, col0, sq, dsq, sdt, rstd, dst_, tmp, dtmp):
        r = rms_rstd([y[:, :W] for y in ys], dys, W, sq, dsq, sdt, rstd, dst_, 1.0 / D)
        for c in range(8):
            tt("vector", tmp[c][:, :W], ys[c][:, :W], r, ALU.mult, [dys[c], dst_], [dtmp[c]])
            stt("vector", tmp[c][:, :W], tmp[c][:, :W], nw(nv, c), xres[c][:, :W], ALU.mult, ALU.add, [dtmp[c], dxres[c], dcst], [dtmp[c]])
            dma("sync", res_out_dram[c * 128:(c + 1) * 128, col0:col0 + W], tmp[c][:, :W], [dtmp[c]], [Dep()])

    phase_no = [0]

    def run_phase(fn):
        phase_no[0] += 1
        pfx = f"p{phase_no[0]}_"
        with contextlib.ExitStack() as es:
            def sb(name, shape, dt):
                return es.enter_context(nc.sbuf_tensor(pfx + name, list(shape), dt))
            fn(sb)
            P.barrier()
            P.replay()

    def run_pipeline(chains, nstage):
        n = len(chains)
        for t in range(n + nstage - 1):
            for st in range(nstage):
                c = t - st
                if 0 <= c < n and st < len(chains[c]) and chains[c][st] is not None:
                    chains[c][st]()

    def phase_A1(sb):
        W = sb("a1w", [128, 32, 1024], BF16)
        Wd = [Dep() for _ in range(32)]
        for oc in range(32):
            dma("gpsimd", W[:, oc, :], gw_in[oc], [], [Wd[oc]])
        wt = sb("a1wt", [128, 8, 16], BF16)
        dwt = Dep()
        dma("gpsimd", wt[:, :, :], gw_tail[:, :, :], [], [dwt])
        cw = sb("a1cw", [128, 24, 4], F32)
        alog = sb("a1alog", [128, 8], F32)
        dtb = sb("a1dtb", [128, 8], F32)
        negA = sb("a1negA", [128, 8], F32)
        posA = sb("a1posA", [128, 8], F32)
        dc = Dep()
        dma("sync", cw[:, :, :], gconv_d[:, :, :], [], [dc])
        dma("sync", alog[:, :], galog_d[:, :], [], [dc])
        dma("sync", dtb[:, :], gdtb_d[:, :], [], [dc])
        act(posA[:, :], alog[:, :], AF.Exp, [dc], [dc])
        vts("vector", negA[:, :], posA[:, :], -1.0, ALU.mult, [dc], [dc])
        halo = sb("a1halo", [128, 24, 3], F32)
        dhalo = [Dep() for _ in range(24)]
        mset("gpsimd", halo[:, :, :], 0.0, dhalo)
        xs = [sb(f"a1x{c}", [128, 512], F32) for c in range(8)]
        dxs = [Dep() for c in range(8)]
        hb = [[sb(f"a1hb{s}_{c}", [128, 512], BF16) for c in range(8)] for s in range(2)]
        dhb = [[Dep() for _ in range(8)] for s in range(2)]
        sq = [sb(f"a1sq{c}", [128, 512], BF16) for c in range(8)]
        dsq = [Dep() for _ in range(8)]
        sdt = sb("a1sdt", [128, 512], F32)
        rstd = sb("a1rstd", [128, 512], F32)
        dst_ = Dep()

        def ring(name, n, shape, dt):
            return [sb(f"{name}{i}", shape, dt) for i in range(n)], [Dep() for _ in range(n)]
        NCB, NACC, NY, NSQ1, NSD1, NOF, NTK = 5, 6, 6, 3, 4, 3, 3
        cbuf, dcb = ring("a1cb", NCB, [128, 515], F32)
        acc, dacc = ring("a1acc", NACC, [128, 512], F32)
        yb, dyb = ring("a1y", NY, [128, 512], F32)
        sq1, dsq1 = ring("a1sq1", NSQ1, [128, 512], BF16)
        sd1, dsd1 = ring("a1sd1", NSD1, [128, 512], F32)
        of, dof = ring("a1of", NOF, [128, 512], F32)
        szt, dszt = ring("a1sz", 2, [128, 512], F32)
        tokf, dtokf = ring("a1tokf", NTK, [128, 4, 128], F32)
        tl = sb("a1tl", [128, 4, 8], F32)
        dtl = [Dep() for _ in range(4)]
        cnt = {"cb": 0, "acc": 0, "y": 0, "sq1": 0, "sd1": 0, "of": 0, "sz": 0, "tk": 0}

        def take(k, n):
            v = cnt[k] % n
            cnt[k] += 1
            return v

        chains = []
        tile_heads = {}
        for j in range(8):
            jb = j % 2
            c0 = j * 512

            def tile_head(j=j, jb=jb, c0=c0):
                for c in range(8):
                    dma("sync", xs[c][:, :], xT[c * 128:(c + 1) * 128, c0:c0 + 512], [], [dxs[c]])
                r = rms_rstd([x[:, :] for x in xs], dxs, 512, sq, dsq, sdt, rstd, dst_, 1.0 / D)
                for c in range(8):
                    stt("vector", hb[jb][c][:, :], xs[c][:, :], nw(0, c), r, ALU.mult, ALU.mult, [dxs[c], dst_, dcst], [dhb[jb][c]])

            tile_heads[j] = tile_head
            for oc in range(32):
                st = {}

                def s0(oc=oc, j=j, jb=jb, st=st, first=(oc == 0 and j == 0), th=tile_head):
                    if first:
                        th()
                    if oc == 20 and j + 1 < 8:
                        tile_heads[j + 1]()
                    b = nb()
                    st["b"] = b
                    for kc in range(8):
                        mm(banks[b][:, :], W[:, oc, kc * 128:(kc + 1) * 128], hb[jb][kc][:, :], kc == 0, kc == 7, [Wd[oc], dhb[jb][kc]], [bdep[b]])

                if oc >= 24:
                    def s1(oc=oc, c0=c0, st=st):
                        b = st["b"]
                        u = take("sz", 2)
                        act(szt[u][:, :], banks[b][:, :], AF.Silu, [], [bdep[b], dszt[u]])
                        dma("sync", sz_s[oc - 24, :, c0:c0 + 512], szt[u][:, :], [dszt[u]], [Dep()])
                    chains.append([s0, s1])
                    continue

                def s1(oc=oc, st=st):
                    b = st["b"]
                    u = take("cb", NCB)
                    a = take("acc", NACC)
                    st["cb"], st["acc"] = u, a
                    cb = cbuf[u]
                    cp("gpsimd", cb[:, 0:3], halo[:, oc, :], [dhalo[oc]], [dcb[u]])
                    act(cb[:, 3:515], banks[b][:, :], AF.Copy, [], [bdep[b], dcb[u]])
                    act(acc[a][:, :], banks[b][:, :], AF.Copy, [dc], [bdep[b], dacc[a]], scale=cw[:, oc, 3:4])
                    cp("gpsimd", halo[:, oc, :], cb[:, 512:515], [dcb[u]], [dhalo[oc]])

                def mk_tap(k, oc=oc, st=st):
                    def f():
                        u, a = st["cb"], st["acc"]
                        stt("vector", acc[a][:, :], cbuf[u][:, k:k + 512], cw[:, oc, k:k + 1], acc[a][:, :], ALU.mult, ALU.add, [dcb[u], dc], [dacc[a]])
                    return f

                def s5(oc=oc, st=st):
                    a = st["acc"]
                    yv = take("y", NY)
                    st["y"] = yv
                    act(yb[yv][:, :], acc[a][:, :], AF.Silu, [dacc[a]], [dyb[yv]])
                    if oc < 16:
                        q1 = take("sq1", NSQ1)
                        st["sq1"] = q1
                        act(sq1[q1][:, :], yb[yv][:, :], AF.Square, [dyb[yv]], [dsq1[q1]])

                def s6(oc=oc, st=st):
                    if oc >= 16:
                        return
                    q1 = st["sq1"]
                    d1 = take("sd1", NSD1)
                    st["sd1"] = d1
                    b2 = nb()
                    mm(banks[b2][:, :], ones_b, sq1[q1][:, :], True, True, [dsq1[q1], dcst], [bdep[b2]])
                    act(sd1[d1][:, :], banks[b2][:, :], AF.Ln, [], [bdep[b2], dsd1[d1]], bias=EPS, scale=1.0)

                def s7(oc=oc, st=st):
                    if oc >= 16:
                        return
                    d1 = st["sd1"]
                    act(sd1[d1][:, :], sd1[d1][:, :], AF.Exp, [dsd1[d1]], [dsd1[d1]], scale=-0.5)

                def s8(oc=oc, st=st):
                    if oc >= 16:
                        return
                    yv, d1 = st["y"], st["sd1"]
                    o_ = take("of", NOF)
                    st["of"] = o_
                    sc = (128.0 ** -0.5) if oc < 8 else 1.0
                    stt("vector", of[o_][:, :], yb[yv][:, :], sc, sd1[d1][:, :], ALU.mult, ALU.mult, [dyb[yv], dsd1[d1]], [dof[o_]])

                def s9(oc=oc, j=j, c0=c0, st=st):
                    h = oc % 8
                    if oc < 16:
                        o_ = st["of"]
                        dst_d = qT_s if oc < 8 else kT_s
                        dma("sync", dst_d[h, :, c0:c0 + 512], of[o_][:, :], [dof[o_]], [Dep()])
                        src, dsrc = of[o_], dof[o_]
                    else:
                        yv = st["y"]
                        src, dsrc = yb[yv], dyb[yv]
                    if oc >= 8:
                        b3 = nb()
                        for i in range(4):
                            tr(banks[b3][:, i * 128:(i + 1) * 128], src[:, i * 128:(i + 1) * 128], ident_f, [dsrc, dcst], [bdep[b3]])
                        tk = take("tk", NTK)
                        if oc < 16:
                            cp("vector", tokf[tk][:, :, :], banks[b3][:, :].rearrange("p (i d) -> p i d", i=4), [], [bdep[b3], dtokf[tk]])
                            dma("sync", ktokf_s[4 * j:4 * j + 4, :, h, :].rearrange("i p d -> p i d"), tokf[tk][:, :, :], [dtokf[tk]], [Dep()])
                        else:
                            cp("scalar", tokf[tk][:, :, :], banks[b3][:, :].rearrange("p (i d) -> p i d", i=4), [], [bdep[b3], dtokf[tk]])
                            dma("sync", vtokf_s[4 * j:4 * j + 4, :, h, :].rearrange("i p d -> p i d"), tokf[tk][:, :, :], [dtokf[tk]], [Dep()])

                chains.append([s0, s1, mk_tap(2), mk_tap(1), mk_tap(0), s5, s6, s7, s8, s9])

            for i in range(4):
                ti = 4 * j + i
                st = {}

                def g0(i=i, jb=jb, st=st):
                    b = nb()
                    st["b"] = b
                    for kc in range(8):
                        mm(banks[b][:, 0:16], hb[jb][kc][:, i * 128:(i + 1) * 128], wt[:, kc, :], kc == 0, kc == 7, [dhb[jb][kc], dwt], [bdep[b]])

                def g1(i=i, ti=ti, st=st):
                    b = st["b"]
                    act(beta_all[:, ti, :], banks[b][:, 0:8], AF.Sigmoid, [], [bdep[b], dsmall])
                    tt("vector", tl[:, i, :], banks[b][:, 8:16], dtb[:, :], ALU.add, [dc], [bdep[b], dtl[i]])
                    vts("gpsimd", nbeta_all[:, ti, :], beta_all[:, ti, :], -1.0, ALU.mult, [dsmall], [dsmall])

                def g2(i=i, st=st):
                    act(tl[:, i, :], tl[:, i, :], AF.Exp, [dtl[i]], [dtl[i]])
                    act(tl[:, i, :], tl[:, i, :], AF.Ln, [dtl[i]], [dtl[i]], bias=1.0)

                def g3(i=i, ti=ti, st=st):
                    tt("vector", g_all[:, ti, :], tl[:, i, :], negA[:, :], ALU.mult, [dtl[i], dc], [dsmall])
                    tt("vector", ng_all[:, ti, :], tl[:, i, :], posA[:, :], ALU.mult, [dtl[i], dc], [dsmall])
                    b2 = nb()
                    st["b2"] = b2
                    mm(banks[b2][:, 0:8], ltri, g_all[:, ti, :], True, True, [dsmall, dcst], [bdep[b2]])
                    mm(banks[b2][:, 8:16], ustr, g_all[:, ti, :], True, True, [dsmall, dcst], [bdep[b2]])
                    mm(banks[b2][:, 16:24], ones_f, g_all[:, ti, :], True, True, [dsmall, dcst], [bdep[b2]])

                def g4(ti=ti, st=st):
                    b2 = st["b2"]
                    act(eg_all[:, ti, :], banks[b2][:, 0:24], AF.Exp, [], [bdep[b2], dsmall])

                chains.append([g0, g1, g2, g3, g4])
        run_pipeline(chains, 10)

    def phase_A2(sb):
        Wo = sb("a2wo", [128, 8, 1024], BF16)
        dWo = Dep()
        for oc in range(8):
            dma("gpsimd", Wo[:, oc, :], gw_out[oc], [], [dWo])
        onw = sb("a2onw", [128, 1], F32)
        donw = Dep()
        dma("sync", onw[:, :], gonorm_d[:, :], [], [donw])
        H8 = range(8)

        def per_head(name, shape, dt):
            return [sb(f"{name}{h}", shape, dt) for h in H8], [Dep() for _ in H8]
        Sf, dSf = per_head("a2Sf", [128, 128], F32)
        for h in H8:
            mset("gpsimd", Sf[h][:, :], 0.0, [dSf[h]])
        qt = [sb(f"a2q{s}", [128, 8, 128], F32) for s in range(2)]
        kt = [sb(f"a2k{s}", [128, 8, 128], F32) for s in range(2)]
        ktk = [sb(f"a2ktk{s}", [128, 8, 128], F32) for s in range(2)]
        vtk = [sb(f"a2vtk{s}", [128, 8, 128], F32) for s in range(2)]
        szt = [sb(f"a2sz{s}", [128, 8, 128], F32) for s in range(2)]
        dld = [Dep() for _ in range(2)]
        junk = sb("a2junk", [128, 128], F32)
        djunk = Dep()
        ogt = sb("a2og", [128, 8, 512], BF16)
        dog = [Dep() for _ in H8]
        ngb = sb("a2ngb", [128, 8, 128], F32)
        dngb = [Dep() for _ in H8]
        t1, dt1 = per_head("a2t1", [128, 128], F32)
        dec, ddec = t1, dt1
        decs, ddecs = per_head("a2decs", [128, 128], F32)
        aqk, daqk = per_head("a2aqk", [128, 128], F32)
        Xa, dXa = per_head("a2Xa", [128, 128], F32)
        Xb, dXb = per_head("a2Xb", [128, 128], F32)
        XTa, dXTa = per_head("a2XTa", [128, 128], F32)
        XTb, dXTb = per_head("a2XTb", [128, 128], F32)
        Ra, dRa = per_head("a2Ra", [128, 128], F32)
        Rb, dRb = per_head("a2Rb", [128, 128], F32)
        kb, dkb = per_head("a2kb", [128, 128], F32)
        kd, dkd = per_head("a2kd", [128, 128], F32)
        wT, dwT = per_head("a2wT", [128, 128], F32)
        ub, dub = per_head("a2ub", [128, 128], F32)
        vn, dvn = per_head("a2vn", [128, 128], F32)
        o1, do1 = per_head("a2o1", [128, 128], F32)
        osb, dosb = o1, do1
        onb, donb = per_head("a2onb", [128, 128], F32)
        st2, dst2 = per_head("a2st", [128, 4], F32)
        ys = [sb(f"a2y{c}", [128, 256], F32) for c in range(8)]
        dys = [Dep() for _ in range(8)]
        sq = [sb(f"a2sq{c}", [128, 256], BF16) for c in range(8)]
        dsq = [Dep() for _ in range(8)]
        sdt = sb("a2sdt", [128, 256], F32)
        rstd = sb("a2rstd", [128, 256], F32)
        dst_ = Dep()
        tmp, dtmp = ys, dys
        xr = [sb(f"a2xr{c}", [128, 256], F32) for c in range(8)]
        dxr = [Dep() for _ in range(8)]

        def loads(ti):
            s = ti % 2
            tc = slice(ti * 128, (ti + 1) * 128)
            dma("sync", qt[s][:, :, :], qT_s[:, :, tc].rearrange("h d t -> d h t"), [], [dld[s]])
            dma("sync", kt[s][:, :, :], kT_s[:, :, tc].rearrange("h d t -> d h t"), [], [dld[s]])
            dma("sync", ktk[s][:, :, :].rearrange("p h d -> p (h d)"), ktokf_s[ti].rearrange("p h d -> p (h d)"), [], [dld[s]])
            dma("sync", vtk[s][:, :, :].rearrange("p h d -> p (h d)"), vtokf_s[ti].rearrange("p h d -> p (h d)"), [], [dld[s]])
            dma("sync", szt[s][:, :, :], sz_s[:, :, tc].rearrange("h e t -> e h t"), [], [dld[s]])

        loads(0)
        for ti in range(32):
            j, i = ti // 4, ti % 4
            s = ti % 2
            c0 = j * 512
            tsl = slice(i * 128, (i + 1) * 128)
            L = [dld[s]]
            if ti + 1 < 32:
                loads(ti + 1)
            B = [banks[h] for h in H8]
            bd = [bdep[h] for h in H8]
            for h in H8:
                cp("gpsimd", ngb[:, h, :], ng_all[:, ti, h:h + 1].to_broadcast([128, 128]), [dsmall], [dngb[h]])
            for h in H8:
                mm(B[h][:, 0:128], g_all[:, ti, h:h + 1].to_broadcast([128, 128]), ltri, True, False, [dsmall, dcst], [bd[h]])
                mm(B[h][:, 0:128], ltri, ngb[:, h, :], False, True, [dngb[h], dcst], [bd[h]])
                mm(B[h][:, 128:256], kt[s][:, h, :], qt[s][:, h, :], True, True, L, [bd[h]])
                mm(B[h][:, 256:384], kt[s][:, h, :], kt[s][:, h, :], True, True, L, [bd[h]])
            for h in H8:
                vts("gpsimd", kb[h][:, :], ktk[s][:, h, :], eg_all[:, ti, h:h + 1], ALU.mult, L + [dsmall], [dkb[h]])
                vts("gpsimd", kd[h][:, :], ktk[s][:, h, :], eg_all[:, ti, 8 + h:9 + h], ALU.mult, L + [dsmall], [dkd[h]])
            for h in H8:
                stt("vector", t1[h][:, :], B[h][:, 0:128], 0.0, negmask, ALU.min, ALU.add, [dcst], [bd[h], dt1[h]])
            for h in H8:
                act(dec[h][:, :], t1[h][:, :], AF.Exp, [dt1[h]], [ddec[h]])
            for h in H8:
                tt("gpsimd", decs[h][:, :], dec[h][:, :], strict01, ALU.mult, [ddec[h], dcst], [ddecs[h]])
            for h in H8:
                tt("vector", aqk[h][:, :], B[h][:, 128:256], dec[h][:, :], ALU.mult, [ddec[h]], [bd[h], daqk[h]])
            for h in H8:
                stt("vector", Xa[h][:, :], B[h][:, 256:384], nbeta_all[:, ti, h:h + 1], decs[h][:, :], ALU.mult, ALU.mult, [ddecs[h], dsmall], [bd[h], dXa[h]])
            for h in H8:
                tt("gpsimd", Ra[h][:, :], Xa[h][:, :], ident_f, ALU.add, [dXa[h], dcst], [dRa[h]])
            for h in H8:
                tr(B[h][:, 0:128], Xa[h][:, :], ident_f, [dXa[h], dcst], [bd[h]])
            for h in H8:
                cp("scalar", XTa[h][:, :], B[h][:, 0:128], [], [bd[h], dXTa[h]])
            Xc, dXc, Xn, dXn = Xa, dXa, Xb, dXb
            XTc, dXTc, XTn, dXTn = XTa, dXTa, XTb, dXTb
            Rc, dRc, Rn, dRn = Ra, dRa, Rb, dRb
            for lev in range(1, 7):
                for h in H8:
                    mm(B[h][:, 0:128], Xc[h][:, :], XTc[h][:, :], True, True, [dXc[h], dXTc[h]], [bd[h]])
                    if lev < 6:
                        mm(B[h][:, 128:256], XTc[h][:, :], Xc[h][:, :], True, True, [dXc[h], dXTc[h]], [bd[h]])
                for h in H8:
                    cp("scalar", XTn[h][:, :], B[h][:, 0:128], [], [bd[h], dXTn[h]])
                if lev < 6:
                    for h in H8:
                        cp("vector", Xn[h][:, :], B[h][:, 128:256], [], [bd[h], dXn[h]])
                for h in H8:
                    mm(B[h][:, 256:384], XTn[h][:, :], Rc[h][:, :], True, True, [dXTn[h], dRc[h]], [bd[h]])
                for h in H8:
                    tt("vector", Rn[h][:, :], B[h][:, 256:384], Rc[h][:, :], ALU.add, [dRc[h]], [bd[h], dRn[h]])
                Xc, dXc, Xn, dXn = Xn, dXn, Xc, dXc
                XTc, dXTc, XTn, dXTn = XTn, dXTn, XTc, dXTc
                Rc, dRc, Rn, dRn = Rn, dRn, Rc, dRc
            for h in H8:
                mm(B[h][:, 0:128], kb[h][:, :], Rc[h][:, :], True, True, [dkb[h], dRc[h]], [bd[h]])
                mm(B[h][:, 128:256], Rc[h][:, :], vtk[s][:, h, :], True, True, L + [dRc[h]], [bd[h]])
            for h in H8:
                cp("scalar", wT[h][:, :], B[h][:, 0:128], [], [bd[h], dwT[h]])
            for h in H8:
                vts("vector", ub[h][:, :], B[h][:, 128:256], beta_all[:, ti, h:h + 1], ALU.mult, [dsmall], [bd[h], dub[h]])
            for h in H8:
                mm(B[h][:, 256:384], wT[h][:, :], Sf[h][:, :], True, True, [dwT[h], dSf[h]], [bd[h]])
                mm(B[h][:, 384:512], qt[s][:, h, :], Sf[h][:, :], True, True, L + [dSf[h]], [bd[h]])
            for h in H8:
                stt("vector", vn[h][:, :], B[h][:, 256:384], nbeta_all[:, ti, h:h + 1], ub[h][:, :], ALU.mult, ALU.add, [dub[h], dsmall], [bd[h], dvn[h]])
            for h in H8:
                vts("vector", o1[h][:, :], B[h][:, 384:512], eg_all[:, ti, h:h + 1], ALU.mult, [dsmall], [bd[h], do1[h]])
            for h in H8:
                mm(B[h][:, 0:128], aqk[h][:, :], vn[h][:, :], True, True, [daqk[h], dvn[h]], [bd[h]])
                mm(B[h][:, 128:256], kd[h][:, :], vn[h][:, :], True, True, [dkd[h], dvn[h]], [bd[h]])
            for h in H8:
                stt("vector", Sf[h][:, :], Sf[h][:, :], eg_all[:, ti, 16 + h:17 + h], B[h][:, 128:256], ALU.mult, ALU.add, [dsmall], [bd[h], dSf[h]])
            for h in H8:
                tt("vector", osb[h][:, :], B[h][:, 0:128], o1[h][:, :], ALU.add, [do1[h]], [bd[h], dosb[h]])
            for h in H8:
                mset("gpsimd", st2[h][:, 0:1], 0.0, [dst2[h]])
            for h in H8:
                act(junk[:, :], osb[h][:, :], AF.Square, [dosb[h]], [djunk, dst2[h]], accum=st2[h][:, 0:1])
            for h in H8:
                act(st2[h][:, 1:2], st2[h][:, 0:1], AF.Sqrt, [dst2[h]], [dst2[h]], bias=EPS, scale=1.0 / 128)
            for h in H8:
                recip(st2[h][:, 2:3], st2[h][:, 1:2], [dst2[h]], [dst2[h]])
            for h in H8:
                vts("gpsimd", onb[h][:, :], osb[h][:, :], st2[h][:, 2:3], ALU.mult, [dosb[h], dst2[h]], [donb[h]])
            for h in H8:
                tr(B[h][:, 256:384], onb[h][:, :], ident_f, [donb[h], dcst], [bd[h]])
            for h in H8:
                stt("vector", ogt[:, h, tsl], B[h][:, 256:384], onw[:, 0:1], szt[s][:, h, :], ALU.mult, ALU.mult, L + [donw], [bd[h], dog[h]])
            if i == 3:
                for hf in range(2):
                    cc = c0 + hf * 256
                    for c in range(8):
                        dma("sync", xr[c][:, :], xT[c * 128:(c + 1) * 128, cc:cc + 256], [], [dxr[c]])
                    for oc in range(8):
                        b = nb()
                        for kc in range(8):
                            mm(banks[b][:, 0:256], Wo[:, oc, kc * 128:(kc + 1) * 128], ogt[:, kc, hf * 256:(hf + 1) * 256], kc == 0, kc == 7, [dWo, dog[kc]], [bdep[b]])
                        cp("scalar", ys[oc][:, :], banks[b][:, 0:256], [], [bdep[b], dys[oc]])
                    post_norm_residual(ys, dys, 256, 2, xr, dxr, R1, cc, sq, dsq, sdt, rstd, dst_, tmp, dtmp)

    def make_ffn(layer, Rin, Rout):
        TW = 256

        def phase(sb):
            Wi = sb("fwi", [128, 44, 1024], BF16)
            dWi = [Dep() for _ in range(44)]
            Wo = sb("fwo", [128, 8, DFF], BF16)
            dWo = [Dep() for _ in range(8)]
            for fc in range(22):
                for oc in (fc, 22 + fc):
                    dma("gpsimd", Wi[:, oc, :], wffn_in[layer, oc], [], [dWi[oc]])
            for oc in range(8):
                dma("gpsimd", Wo[:, oc, :], wffn_out[layer, oc], [], [dWo[oc]])
            xs = [[sb(f"fx{s}_{c}", [128, TW], F32) for c in range(8)] for s in range(2)]
            dxs = [[Dep() for c in range(8)] for s in range(2)]
            hb2 = [[sb(f"fhb{s}_{c}", [128, TW], BF16) for c in range(8)] for s in range(2)]
            dhb2 = [[Dep() for _ in range(8)] for s in range(2)]
            sq = [sb(f"fsq{c}", [128, TW], BF16) for c in range(8)]
            dsq = [Dep() for _ in range(8)]
            sdt = sb("fsdt", [128, TW], F32)
            rstd = sb("frstd", [128, TW], F32)
            dst_ = Dep()
            sqi = [sb(f"fsqi{c}", [128, TW], BF16) for c in range(8)]
            dsqi = [Dep() for _ in range(8)]
            sdti = sb("fsdti", [128, TW], F32)
            rstdi = sb("frstdi", [128, TW], F32)
            dsti = Dep()
            sg = [sb(f"fsg{i}", [128, TW], F32) for i in range(2)]
            dsg = [Dep() for _ in range(2)]
            at = [sb(f"fa{f}", [128, TW], BF16) for f in range(22)]
            dat = [Dep() for _ in range(22)]
            ys = [sb(f"fy{c}", [128, TW], F32) for c in range(8)]
            dys = [Dep() for _ in range(8)]
            tmp, dtmp = ys, dys
            def norm_in(j):
                s = j % 2
                c0 = j * TW
                for c in range(8):
                    dma("sync", xs[s][c][:, :], Rin[c * 128:(c + 1) * 128, c0:c0 + TW], [], [dxs[s][c]])
                r = rms_rstd([x[:, :] for x in xs[s]], dxs[s], TW, sqi, dsqi, sdti, rstdi, dsti, 1.0 / D)
                for c in range(8):
                    stt("vector", hb2[s][c][:, :], xs[s][c][:, :], nw(4 + layer, c), r, ALU.mult, ALU.mult, [dxs[s][c], dsti, dcst], [dhb2[s][c]])

            NTL = T // TW
            norm_in(0)
            for j in range(NTL):
                s = j % 2
                c0 = j * TW
                hb, dhb = hb2[s], dhb2[s]
                for fc in range(22):
                    bg = nb()
                    for kc in range(8):
                        mm(banks[bg][:, :TW], Wi[:, fc, kc * 128:(kc + 1) * 128], hb[kc][:, :], kc == 0, kc == 7, [dWi[fc], dhb[kc]], [bdep[bg]])
                    bu = nb()
                    for kc in range(8):
                        mm(banks[bu][:, :TW], Wi[:, 22 + fc, kc * 128:(kc + 1) * 128], hb[kc][:, :], kc == 0, kc == 7, [dWi[22 + fc], dhb[kc]], [bdep[bu]])
                    u = fc % 2
                    act(sg[u][:, :], banks[bg][:, :TW], AF.Silu, [], [bdep[bg], dsg[u]])
                    tt("vector", at[fc][:, :], banks[bu][:, :TW], sg[u][:, :], ALU.mult, [dsg[u]], [bdep[bu], dat[fc]])
                if j + 1 < NTL:
                    norm_in(j + 1)
                for oc in range(8):
                    b = nb()
                    for fc in range(22):
                        mm(banks[b][:, :TW], Wo[:, oc, fc * 128:(fc + 1) * 128], at[fc][:, :], fc == 0, fc == 21, [dWo[oc], dat[fc]], [bdep[b]])
                    cp("scalar", ys[oc][:, :], banks[b][:, :TW], [], [bdep[b], dys[oc]])
                post_norm_residual(ys, dys, TW, 6 + layer, xs[s], dxs[s], Rout, c0, sq, dsq, sdt, rstd, dst_, tmp, dtmp)
        return phase

    def phase_B01(sb):
        Wk = sb("bwk", [128, 8, 1024], BF16)
        Wq = sb("bwq", [128, 8, 1024], BF16)
        Wv = sb("bwv", [128, 8, 1024], BF16)
        Wf = sb("bwf", [128, 8, 16], BF16)
        dW = Dep()
        for oc in range(8):
            dma("gpsimd", Wk[:, oc, :], wk_d[oc], [], [dW])
            dma("gpsimd", Wq[:, oc, :], wq_d[oc], [], [dW])
        dma("gpsimd", Wv[:, :, :], wv_d[:, :, :], [], [dW])
        dma("gpsimd", Wf[:, :, :], wf_d[:, :, :], [], [dW])
        bfor = sb("bbfor", [128, 16], F32)
        dma("sync", bfor[:, :], bfor_d[:, :], [], [dW])
        ones3 = sb("bones3", [128, T], BF16)
        done3 = Dep()
        mset("gpsimd", ones3[:, :], 1.0, [done3])
        for h in range(16):
            dma("sync", kx_s[h, 64:67, :], ones3[0:3, :], [done3], [Dep()])
        carry = sb("bcarry", [128, 16], F32)
        dcarry = Dep()
        mset("gpsimd", carry[:, :], 0.0, [dcarry])
        xs = [[sb(f"bx{s}_{c}", [128, 512], F32) for c in range(8)] for s in range(2)]
        dxs = [[Dep() for c in range(8)] for s in range(2)]
        hk2 = [[sb(f"bhk{s}_{c}", [128, 512], BF16) for c in range(8)] for s in range(2)]
        dhk2 = [[Dep() for _ in range(8)] for s in range(2)]
        hq2 = [[sb(f"bhq{s}_{c}", [128, 512], BF16) for c in range(8)] for s in range(2)]
        dhq2 = [[Dep() for _ in range(8)] for s in range(2)]
        sq = [sb(f"bsq{c}", [128, 512], BF16) for c in range(8)]
        dsq = [Dep() for _ in range(8)]
        sdt = sb("bsdt", [128, 512], F32)
        rstd = sb("brstd", [128, 512], F32)
        dst_ = Dep()
        ob = [sb(f"bob{i}", [128, 512], BF16) for i in range(2)]
        dob = [Dep() for _ in range(2)]
        vx = [sb(f"bvx{i}", [128, 16, 65], BF16) for i in range(2)]
        dvx = [Dep() for _ in range(2)]
        for i in range(2):
            mset("gpsimd", vx[i][:, :, :], 1.0, [dvx[i]])
        tl = sb("btl", [128, 4, 16], F32)
        dtl = Dep()
        crow = sb("bcrow", [16, 512], F32)
        r1 = sb("br1", [16, 512], F32)
        chi = sb("bchi", [16, 3, 512], BF16)
        dcr = Dep()
        cnt = [0]
        def norm_in(j):
            s = j % 2
            c0 = j * 512
            for c in range(8):
                dma("sync", xs[s][c][:, :], R2[c * 128:(c + 1) * 128, c0:c0 + 512], [], [dxs[s][c]])
            r = rms_rstd([x[:, :] for x in xs[s]], dxs[s], 512, sq, dsq, sdt, rstd, dst_, 1.0 / D)
            for c in range(8):
                stt("vector", hk2[s][c][:, :], xs[s][c][:, :], nw(8, c), r, ALU.mult, ALU.mult, [dxs[s][c], dst_, dcst], [dhk2[s][c]])
                stt("vector", hq2[s][c][:, :], xs[s][c][:, :], nw(1, c), r, ALU.mult, ALU.mult, [dxs[s][c], dst_, dcst], [dhq2[s][c]])

        norm_in(0)
        for j in range(8):
            s = j % 2
            c0 = j * 512
            hk, dhk, hq, dhq = hk2[s], dhk2[s], hq2[s], dhq2[s]
            for (Wm, hsrc, dh, dstd, scl) in ((Wk, hk, dhk, kx_s, 1.0), (Wq, hq, dhq, qx_s, 0.125)):
                for oc in range(8):
                    b = nb()
                    for kc in range(8):
                        mm(banks[b][:, :], Wm[:, oc, kc * 128:(kc + 1) * 128], hsrc[kc][:, :], kc == 0, kc == 7, [dW, dh[kc]], [bdep[b]])
                    u = cnt[0] % 2
                    cnt[0] += 1
                    vts("vector", ob[u][:, :], banks[b][:, :], scl, ALU.mult, [], [bdep[b], dob[u]])
                    dma("sync", dstd[2 * oc, 0:64, c0:c0 + 512], ob[u][0:64, :], [dob[u]], [Dep()])
                    dma("sync", dstd[2 * oc + 1, 0:64, c0:c0 + 512], ob[u][64:128, :], [dob[u]], [Dep()])
            if j + 1 < 8:
                norm_in(j + 1)
            for i in range(4):
                ti = 4 * j + i
                tsl = slice(i * 128, (i + 1) * 128)
                u = ti % 2
                for half in range(2):
                    b = nb()
                    for kc in range(8):
                        mm(banks[b][:, :], hk[kc][:, tsl], Wv[:, kc, half * 512:(half + 1) * 512], kc == 0, kc == 7, [dW, dhk[kc]], [bdep[b]])
                    if half == 0:
                        cp("vector", vx[u][:, 0:8, 0:64], banks[b][:, :].rearrange("p (h e) -> p h e", h=8), [], [bdep[b], dvx[u]])
                    else:
                        cp("scalar", vx[u][:, 8:16, 0:64], banks[b][:, :].rearrange("p (h e) -> p h e", h=8), [], [bdep[b], dvx[u]])
                dma("sync", vx_s[ti].rearrange("p h e -> p (h e)"), vx[u][:, :, :].rearrange("p h e -> p (h e)"), [dvx[u]], [Dep()])
                b = nb()
                for kc in range(8):
                    mm(banks[b][:, 0:16], hk[kc][:, tsl], Wf[:, kc, :], kc == 0, kc == 7, [dW, dhk[kc]], [bdep[b]])
                tt("vector", tl[:, i, :], banks[b][:, 0:16], bfor[:, :], ALU.add, [dW], [bdep[b], dtl])
                act(tl[:, i, :], tl[:, i, :], AF.Exp, [dtl], [dtl], scale=-1.0)
                act(tl[:, i, :], tl[:, i, :], AF.Ln, [dtl], [dtl], bias=1.0)
                b2 = nb()
                mm(banks[b2][:, 0:16], ltri, tl[:, i, :], True, True, [dtl, dcst], [bdep[b2]])
                mm(banks[b2][:, 16:32], ones_f, tl[:, i, :], True, True, [dtl, dcst], [bdep[b2]])
                tt("vector", negc_all[:, ti, :], banks[b2][:, 0:16], carry[:, :], ALU.add, [dcarry], [bdep[b2], dsmall])
                tt("vector", carry[:, :], banks[b2][:, 16:32], carry[:, :], ALU.add, [], [bdep[b2], dcarry])
                b3 = nb()
                tr(banks[b3][0:16, 0:128], negc_all[:, ti, :], ident_f, [dsmall, dcst], [bdep[b3]])
                vts("vector", crow[:, tsl], banks[b3][0:16, 0:128], -1.0, ALU.mult, [], [bdep[b3], dcr])
            cp("vector", chi[:, 0, :], crow[:, :], [dcr], [dcr])
            tt("vector", r1[:, :], crow[:, :], chi[:, 0, :], ALU.subtract, [dcr], [dcr])
            cp("vector", chi[:, 1, :], r1[:, :], [dcr], [dcr])
            tt("vector", r1[:, :], r1[:, :], chi[:, 1, :], ALU.subtract, [dcr], [dcr])
            cp("vector", chi[:, 2, :], r1[:, :], [dcr], [dcr])
            for q3 in range(3):
                dma("sync", qx_s[:, 64 + q3, c0:c0 + 512], chi[:, q3, :], [dcr], [Dep()])

    def phase_B2(sb):
        kx = [sb(f"ckx{i}", [128, T], BF16) for i in range(2)]
        qx = [sb(f"cqx{i}", [128, T], BF16) for i in range(2)]
        vx = [sb(f"cvx{i}", [128, 32, 65], BF16) for i in range(2)]
        dld = [Dep() for _ in range(2)]
        NPT = 8
        LA = 4
        pt = [sb(f"cpt{i}", [128, 512], BF16) for i in range(NPT)]
        dpt = [Dep() for _ in range(NPT)]
        rlt = [sb(f"crl{i}", [128, 512], F32) for i in range(2)]
        drl = [Dep() for _ in range(2)]
        bc = sb("cbc", [64, 512], F32)
        dbc = Dep()
        ot = [sb(f"cot{i}", [64, 512], BF16) for i in range(2)]
        dot = [Dep() for _ in range(2)]
        ACC = [0, 1]
        SBK = [2, 3, 4, 5, 7]
        NBK = 6
        blocks = []
        for h in range(16):
            for jq in range(8):
                for ik in range(4 * jq + 4):
                    blocks.append((h, jq, ik))
        n = len(blocks)

        def head_loads(h):
            s = h % 2
            dma("sync", kx[s][0:67, :], kx_s[h, :, :], [], [dld[s]])
            dma("sync", qx[s][0:67, :], qx_s[h, :, :], [], [dld[s]])
            for g4 in range(8):
                dma("sync", vx[s][:, g4 * 4:(g4 + 1) * 4, :], vx_s[g4 * 4:(g4 + 1) * 4, :, h, :].rearrange("t p e -> p t e"), [], [dld[s]])

        def s_side(idx):
            h, jq, ik = blocks[idx]
            s = h % 2
            if h == 0 and jq == 0 and ik == 0:
                head_loads(0)
            if jq == 5 and ik == 0 and h + 1 < 16:
                head_loads(h + 1)
            diag = ik - 4 * jq
            t0 = max(0, diag) * 128
            bs = SBK[idx % 5]
            u = idx % NPT
            mm(banks[bs][:, t0:512], kx[s][0:67, ik * 128:(ik + 1) * 128], qx[s][0:67, jq * 512 + t0:(jq + 1) * 512], True, True, [dld[s]], [bdep[bs]])
            act(pt[u][:, t0:512], banks[bs][:, t0:512], AF.Exp, [dsmall], [bdep[bs], dpt[u]], bias=negc_all[:, ik, h:h + 1])
            if diag >= 0:
                tt("gpsimd", pt[u][:, t0:t0 + 128], pt[u][:, t0:t0 + 128], incl01_b, ALU.mult, [dcst], [dpt[u]])

        def pv_side(idx):
            h, jq, ik = blocks[idx]
            s = h % 2
            nk = 4 * jq + 4
            diag = ik - 4 * jq
            t0 = max(0, diag) * 128
            u = idx % NPT
            ba = ACC[jq % 2]
            mm(banks[ba][0:65, t0:512], vx[s][:, ik, :], pt[u][:, t0:512], ik == 0, ik == nk - 1, [dld[s], dpt[u]], [bdep[ba]])
            if ik == nk - 1:
                r2 = jq % 2
                cp("vector", rlt[r2][0:65, :], banks[ba][0:65, :], [], [bdep[ba], drl[r2]])
                mm(banks[NBK][0:64, :], ident_f[0:65, 64:65].to_broadcast([65, 64]), rlt[r2][0:65, :], True, True, [drl[r2], dcst], [bdep[NBK]])
                recip(bc[:, :], banks[NBK][0:64, :], [], [bdep[NBK], dbc])
                tt("vector", ot[r2][:, :], rlt[r2][0:64, :], bc[:, :], ALU.mult, [dbc, drl[r2]], [dot[r2]])
                dma("sync", at_s[h * 64:(h + 1) * 64, jq * 512:(jq + 1) * 512], ot[r2][:, :], [dot[r2]], [Dep()])

        for idx in range(n + LA):
            if idx < n:
                s_side(idx)
            if idx >= LA:
                pv_side(idx - LA)

    def phase_B3(sb):
        Wo = sb("dwo", [128, 8, 1024], BF16)
        dWo = Dep()
        for oc in range(8):
            dma("gpsimd", Wo[:, oc, :], wo_d[oc], [], [dWo])
        at = [[sb(f"dat{s}_{c}", [128, 512], BF16) for c in range(8)] for s in range(2)]
        dat = [[Dep() for c in range(8)] for s in range(2)]
        xr = [sb(f"dxr{c}", [128, 512], F32) for c in range(8)]
        dxr = [Dep() for _ in range(8)]
        ys = [sb(f"dy{c}", [128, 512], F32) for c in range(8)]
        dys = [Dep() for _ in range(8)]
        sq = [sb(f"dsq{c}", [128, 512], BF16) for c in range(8)]
        dsq = [Dep() for _ in range(8)]
        sdt = sb("dsdt", [128, 512], F32)
        rstd = sb("drstd", [128, 512], F32)
        dst_ = Dep()
        tmp = [sb(f"dtmp{c}", [128, 512], F32) for c in range(8)]
        dtmp = [Dep() for _ in range(8)]
        for j in range(8):
            s = j % 2
            c0 = j * 512
            for c in range(8):
                dma("sync", at[s][c][:, :], at_s[c * 128:(c + 1) * 128, c0:c0 + 512], [], [dat[s][c]])
                dma("sync", xr[c][:, :], R2[c * 128:(c + 1) * 128, c0:c0 + 512], [], [dxr[c]])
            for oc in range(8):
                b = nb()
                for kc in range(8):
                    mm(banks[b][:, :], Wo[:, oc, kc * 128:(kc + 1) * 128], at[s][kc][:, :], kc == 0, kc == 7, [dWo, dat[s][kc]], [bdep[b]])
                cp("scalar", ys[oc][:, :], banks[b][:, :], [], [bdep[b], dys[oc]])
            post_norm_residual(ys, dys, 512, 3, xr, dxr, R3, c0, sq, dsq, sdt, rstd, dst_, tmp, dtmp)

    phases = {
        "A1": phase_A1,
        "A2": phase_A2,
        "F0": make_ffn(0, R1, R2),
        "B01": phase_B01,
        "B2": phase_B2,
        "B3": phase_B3,
        "F1": make_ffn(1, R3, outT),
    }
    for name in PHASES:
        run_phase(phases[name])
        if stop_after == name:
            break
    print(f"[kernel] built program: {P.nops} ops, {len(P.sems)} semaphores")
    return nc


def _oc_tiles(w):
    K, N = w.shape
    return np.ascontiguousarray(w.reshape(K // 128, 128, N // 128, 128).transpose(2, 1, 0, 3).reshape(N // 128, 128, (K // 128) * 128))


def _mov_tiles(w):
    K, N = w.shape
    return np.ascontiguousarray(w.reshape(K // 128, 128, N).transpose(1, 0, 2))


def _consts():
    s = np.arange(128)[:, None]
    c = np.arange(128)[None, :]
    ident = (s == c).astype(np.float32)
    ltri = (s <= c).astype(np.float32)
    ustr = (s > c).astype(np.float32)
    ones = np.ones((128, 128), np.float32)
    negmask = np.where(c >= s, 0.0, -30000.0).astype(np.float32)
    strict01 = (c > s).astype(np.float32)
    incl01 = (c >= s).astype(np.float32)
    return np.ascontiguousarray(np.stack([ident, ltri, ustr, ones, negmask, strict01, incl01], axis=1))


def prepare_inputs(x, pre_mix_norm, post_mix_norm, pre_ffn_norm, post_ffn_norm, w_ffn_in, w_ffn_out,
                   gdn_w_in, gdn_conv, gdn_a_log, gdn_dt_bias, gdn_out_norm, gdn_w_out,
                   kv_norm, w_kv, b_forget, fox_w_q, fox_w_o):
    f = lambda a: np.asarray(a, dtype=np.float32)
    x = f(x)
    vecs = [f(pre_mix_norm)[0], f(pre_mix_norm)[1], f(post_mix_norm)[0], f(post_mix_norm)[1],
            f(pre_ffn_norm)[0], f(pre_ffn_norm)[1], f(post_ffn_norm)[0], f(post_ffn_norm)[1], f(kv_norm)]
    nrm = np.ascontiguousarray(np.concatenate([v.reshape(8, 128).T for v in vecs], axis=1))
    gwin = f(gdn_w_in)[0]
    shared = {
        "nrm": nrm,
        "gw_in": _oc_tiles(gwin[:, :4096]),
        "gw_tail": _mov_tiles(gwin[:, 4096:4112]),
        "gconv": np.ascontiguousarray(f(gdn_conv)[0].reshape(4, 24, 128).transpose(2, 1, 0)),
        "galog": np.ascontiguousarray(np.broadcast_to(f(gdn_a_log)[0][None, :], (128, 8))),
        "gdtb": np.ascontiguousarray(np.broadcast_to(f(gdn_dt_bias)[0][None, :], (128, 8))),
        "gonorm": np.ascontiguousarray(f(gdn_out_norm)[0].reshape(128, 1)),
        "gw_out": _oc_tiles(f(gdn_w_out)[0]),
        "wffn_in": np.stack([_oc_tiles(f(w_ffn_in)[l]) for l in range(2)]),
        "wffn_out": np.stack([_oc_tiles(f(w_ffn_out)[l]) for l in range(2)]),
        "wk": _oc_tiles(f(w_kv)[:, :1024]),
        "wv": _mov_tiles(f(w_kv)[:, 1024:2048]),
        "wf": _mov_tiles(f(w_kv)[:, 2048:2064]),
        "bfor": np.ascontiguousarray(np.broadcast_to(f(b_forget)[None, :], (128, 16))),
        "wq": _oc_tiles(f(fox_w_q)[0]),
        "wo": _oc_tiles(f(fox_w_o)[0]),
        "cst": _consts(),
    }
    in_maps = []
    for b in range(NCORES):
        m = dict(shared)
        m["xT"] = np.ascontiguousarray(x[b].T)
        in_maps.append(m)
    return in_maps


def kernel(**inputs):
    in_maps = prepare_inputs(**inputs)
    nc = build_program()
    res = run_bass_kernel_spmd(nc, in_maps, core_ids=list(range(NCORES)))
    out = np.stack([np.ascontiguousarray(r["outT"].T) for r in res.results], axis=0)
    return out.astype(np.float32)
```
